# Optimizing a Trainium2 kernel written in Bass

```python
import math
import jax, jax.numpy as jnp
from jax import lax
import numpy as np

D_MODEL = 1024
BATCH = 16
SEQ = 2048
DEPTH = 1

MIX_WIDTH = D_MODEL
ATTN_WIDTH = MIX_WIDTH // 2
REC_WIDTH = MIX_WIDTH - ATTN_WIDTH
DA_HEAD_DIM = 64
DA_V_DIM = 2 * DA_HEAD_DIM
DA_N_HEADS = ATTN_WIDTH // DA_V_DIM
REC_BLOCKS = 8
REC_BLOCK_DIM = REC_WIDTH // REC_BLOCKS
CONV_WIDTH = 4
LRU_C = 8.0
D_FF = -(-8 * D_MODEL // (3 * 256)) * 256
Q_BLOCK = 128
IN_WIDTH = 3 * ATTN_WIDTH + 2 * REC_WIDTH
DEEPNORM_ALPHA = (2.0 * DEPTH) ** 0.25
DEEPNORM_BETA = (8.0 * DEPTH) ** -0.25
LN_EPS = 1e-5
RMS_EPS = 1e-5

kernel_name = "hymba_style_diffattn_rglru_deepnorm"


def layer_norm(x, g, b):
    xf = x.astype(jnp.float32)
    mu = jnp.mean(xf, axis=-1, keepdims=True)
    xc = xf - mu
    var = jnp.mean(xc * xc, axis=-1, keepdims=True)
    return (xc * lax.rsqrt(var + LN_EPS) * g.astype(jnp.float32) + b.astype(jnp.float32)).astype(x.dtype)


def rms_norm(x, g):
    xf = x.astype(jnp.float32)
    ms = jnp.mean(xf * xf, axis=-1, keepdims=True)
    return (xf * lax.rsqrt(ms + RMS_EPS) * g.astype(jnp.float32)).astype(x.dtype)


def diff_lambda(lam_params, lambda_init):
    lp = lam_params.astype(jnp.float32)
    return jnp.exp(jnp.sum(lp[0] * lp[1])) - jnp.exp(jnp.sum(lp[2] * lp[3])) + lambda_init


def differential_attention(q, k, v, lam, sub_g, lambda_init):
    B, S = q.shape[0], q.shape[1]
    q = q * (DA_HEAD_DIM ** -0.5)
    outs = []
    for blk in range(S // Q_BLOCK):
        q0 = blk * Q_BLOCK
        kend = q0 + Q_BLOCK
        qb = q[:, q0:kend]
        kb = k[:, :kend]
        vb = v[:, :kend]
        s = jnp.einsum('bqhcd,bkhcd->bhcqk', qb, kb).astype(jnp.float32)
        mask = (q0 + jnp.arange(Q_BLOCK))[:, None] >= jnp.arange(kend)[None, :]
        s = jnp.where(mask, s, -jnp.inf)
        p = jax.nn.softmax(s, axis=-1)
        w = p[:, :, 0] - lam * p[:, :, 1]
        outs.append(jnp.einsum('bhqk,bkhe->bqhe', w.astype(v.dtype), vb))
    o = jnp.concatenate(outs, axis=1)
    o = rms_norm(o, sub_g) * (1.0 - lambda_init)
    return o.reshape(B, S, DA_N_HEADS * DA_V_DIM)


def causal_depthwise_conv(x, w, b):
    S = x.shape[1]
    xp = jnp.pad(x, ((0, 0), (CONV_WIDTH - 1, 0), (0, 0)))
    out = b
    for tap in range(CONV_WIDTH):
        out = out + xp[:, tap:tap + S] * w[tap]
    return out


def rg_lru(x, w_a, b_a, w_x, b_x, lam):
    B, S, C = x.shape
    xf = x.astype(jnp.float32)
    xb = xf.reshape(B, S, REC_BLOCKS, REC_BLOCK_DIM)
    r = jax.nn.sigmoid(jnp.einsum('bsnd,nde->bsne', xb, w_a.astype(jnp.float32)).reshape(B, S, C) + b_a)
    i = jax.nn.sigmoid(jnp.einsum('bsnd,nde->bsne', xb, w_x.astype(jnp.float32)).reshape(B, S, C) + b_x)
    log_a = -LRU_C * r * jax.nn.softplus(-lam.astype(jnp.float32))
    a = jnp.exp(log_a)
    mult = jnp.sqrt(-jnp.expm1(2.0 * log_a))
    first = (jnp.arange(S) == 0)[None, :, None]
    mult = jnp.where(first, 1.0, mult)
    bterm = mult * (i * xf)

    def combine(e1, e2):
        a1, b1 = e1
        a2, b2 = e2
        return a1 * a2, a2 * b1 + b2

    _, h = lax.associative_scan(combine, (a, bterm), axis=1)
    return h.astype(x.dtype)


def hybrid_mixer(x, w_in, conv_w, conv_b, w_a, b_a, w_x, b_x, lru_lam, rec_g,
                 da_lam, da_g, w_out, lambda_init):
    B, S, _ = x.shape
    proj = jnp.einsum('bsd,de->bse', x, w_in)
    q, k, v, xr, gate = jnp.split(
        proj, [ATTN_WIDTH, 2 * ATTN_WIDTH, 3 * ATTN_WIDTH, 3 * ATTN_WIDTH + REC_WIDTH], axis=-1)
    q = q.reshape(B, S, DA_N_HEADS, 2, DA_HEAD_DIM)
    k = k.reshape(B, S, DA_N_HEADS, 2, DA_HEAD_DIM)
    v = v.reshape(B, S, DA_N_HEADS, DA_V_DIM)
    lam = diff_lambda(da_lam, lambda_init)
    attn_out = differential_attention(q, k, v, lam, da_g, lambda_init)
    xr = causal_depthwise_conv(xr, conv_w, conv_b)
    h = rg_lru(xr, w_a, b_a, w_x, b_x, lru_lam)
    rec_out = rms_norm(h * jax.nn.gelu(gate, approximate=True), rec_g)
    merged = jnp.concatenate([attn_out, rec_out], axis=-1)
    return jnp.einsum('bse,ed->bsd', merged, w_out)


def swiglu_ffn(x, w_ffn_in, w_ffn_out):
    hu = jnp.einsum('bsd,df->bsf', x, w_ffn_in)
    g, u = jnp.split(hu, 2, axis=-1)
    return jnp.einsum('bsf,fd->bsd', jax.nn.silu(g) * u, w_ffn_out)


def setup_inputs(seed: int = 0) -> dict:
    key = jax.random.key(seed)
    ks = jax.random.split(key, 20)
    f32 = jnp.float32
    x = jax.random.normal(ks[0], (BATCH, SEQ, D_MODEL), f32)
    col_scale = jnp.concatenate([
        jnp.ones((2 * ATTN_WIDTH,), f32),
        jnp.full((ATTN_WIDTH + REC_WIDTH,), DEEPNORM_BETA, f32),
        jnp.ones((REC_WIDTH,), f32)])
    w_in = jax.random.normal(ks[1], (DEPTH, D_MODEL, IN_WIDTH), f32) * (D_MODEL ** -0.5) * col_scale
    conv_w = jax.random.normal(ks[2], (DEPTH, CONV_WIDTH, REC_WIDTH), f32) * (CONV_WIDTH ** -0.5)
    conv_b = 0.01 * jax.random.normal(ks[3], (DEPTH, REC_WIDTH), f32)
    lru_w_a = jax.random.normal(ks[4], (DEPTH, REC_BLOCKS, REC_BLOCK_DIM, REC_BLOCK_DIM), f32) * (REC_BLOCK_DIM ** -0.5)
    lru_b_a = 0.01 * jax.random.normal(ks[5], (DEPTH, REC_WIDTH), f32)
    lru_w_x = jax.random.normal(ks[6], (DEPTH, REC_BLOCKS, REC_BLOCK_DIM, REC_BLOCK_DIM), f32) * (REC_BLOCK_DIM ** -0.5)
    lru_b_x = 0.01 * jax.random.normal(ks[7], (DEPTH, REC_WIDTH), f32)
    u = jax.random.uniform(ks[8], (DEPTH, REC_WIDTH), f32, 0.9, 0.999)
    s = u ** (1.0 / LRU_C)
    lru_lambda = jnp.log(s) - jnp.log1p(-s)
    rec_norm_g = 1.0 + 0.02 * jax.random.normal(ks[9], (DEPTH, REC_WIDTH), f32)
    da_lambda = 0.1 * jax.random.normal(ks[10], (DEPTH, 4, DA_HEAD_DIM), f32)
    da_norm_g = 1.0 + 0.02 * jax.random.normal(ks[11], (DEPTH, DA_V_DIM), f32)
    w_out = jax.random.normal(ks[12], (DEPTH, MIX_WIDTH, D_MODEL), f32) * (MIX_WIDTH ** -0.5) * DEEPNORM_BETA
    ln1_g = 1.0 + 0.02 * jax.random.normal(ks[13], (DEPTH, D_MODEL), f32)
    ln1_b = 0.02 * jax.random.normal(ks[14], (DEPTH, D_MODEL), f32)
    w_ffn_in = jax.random.normal(ks[15], (DEPTH, D_MODEL, 2 * D_FF), f32) * (D_MODEL ** -0.5) * DEEPNORM_BETA
    w_ffn_out = jax.random.normal(ks[16], (DEPTH, D_FF, D_MODEL), f32) * (D_FF ** -0.5) * DEEPNORM_BETA
    ln2_g = 1.0 + 0.02 * jax.random.normal(ks[17], (DEPTH, D_MODEL), f32)
    ln2_b = 0.02 * jax.random.normal(ks[18], (DEPTH, D_MODEL), f32)
    return {"x": x, "w_in": w_in, "conv_w": conv_w, "conv_b": conv_b,
            "lru_w_a": lru_w_a, "lru_b_a": lru_b_a, "lru_w_x": lru_w_x, "lru_b_x": lru_b_x,
            "lru_lambda": lru_lambda, "rec_norm_g": rec_norm_g, "da_lambda": da_lambda,
            "da_norm_g": da_norm_g, "w_out": w_out, "ln1_g": ln1_g, "ln1_b": ln1_b,
            "w_ffn_in": w_ffn_in, "w_ffn_out": w_ffn_out, "ln2_g": ln2_g, "ln2_b": ln2_b}


def reference(x, w_in, conv_w, conv_b, lru_w_a, lru_b_a, lru_w_x, lru_b_x, lru_lambda,
              rec_norm_g, da_lambda, da_norm_g, w_out, ln1_g, ln1_b, w_ffn_in, w_ffn_out,
              ln2_g, ln2_b):
    for l in range(DEPTH):
        lambda_init = 0.8 - 0.6 * math.exp(-0.3 * l)
        mix = hybrid_mixer(x, w_in[l], conv_w[l], conv_b[l], lru_w_a[l], lru_b_a[l],
                           lru_w_x[l], lru_b_x[l], lru_lambda[l], rec_norm_g[l],
                           da_lambda[l], da_norm_g[l], w_out[l], lambda_init)
        x = layer_norm(DEEPNORM_ALPHA * x + mix, ln1_g[l], ln1_b[l])
        ffn = swiglu_ffn(x, w_ffn_in[l], w_ffn_out[l])
        x = layer_norm(DEEPNORM_ALPHA * x + ffn, ln2_g[l], ln2_b[l])
    return x
```

```python
import numpy as np
import concourse.bass as bass
import concourse.mybir as mybir
from concourse.bass_utils import run_bass_kernel_spmd

F32 = mybir.dt.float32
BF16 = mybir.dt.bfloat16
AF = mybir.ActivationFunctionType
ALU = mybir.AluOpType
AX = mybir.AxisListType

NCORES = 8
NSEQ = 2
S = 2048
D = 1024
DFF = 2816
NJ = DFF // 128
ALPHA = float(2.0 ** 0.25)
LAMBDA_INIT = 0.2
LN_EPS = 1e-5
RMS_EPS = 1e-5


class _Op:
    __slots__ = ("eng", "fn", "idx", "dma", "h", "deps", "sig")


class Prog:
    ENGS = ("pe", "act", "dve", "pool", "sp")

    def __init__(self):
        self.q = {e: [] for e in self.ENGS}
        self.tok = {}
        self.dma_n = {}

    def _grp(self, g):
        return self.tok.setdefault(g, {"sw": None, "sr": [], "subs": {}})

    def add(self, eng, fn, reads=(), writes=(), dma=None):
        op = _Op()
        op.eng = eng
        op.fn = fn
        op.idx = len(self.q[eng])
        op.dma = dma
        op.sig = False
        if dma is not None:
            n = self.dma_n.get(dma, 0) + 1
            self.dma_n[dma] = n
            op.h = ("d", dma, n)
        else:
            op.h = ("c", eng, op.idx)
        deps = set()
        for t in reads:
            g = self._grp(t[0])
            sub = tuple(t[1:])
            if sub == ("*",):
                for ent in g["subs"].values():
                    if ent[0] is not None:
                        deps.add(ent[0])
                if g["sw"] is not None:
                    deps.add(g["sw"])
                g["sr"].append(op.h)
            else:
                ent = g["subs"].get(sub)
                if ent is None:
                    ent = g["subs"][sub] = [None, []]
                if ent[0] is not None:
                    deps.add(ent[0])
                elif g["sw"] is not None:
                    deps.add(g["sw"])
                ent[1].append(op.h)
        for t in writes:
            g = self._grp(t[0])
            sub = tuple(t[1:])
            if sub == ("*",):
                for ent in g["subs"].values():
                    if ent[0] is not None:
                        deps.add(ent[0])
                    deps.update(ent[1])
                if g["sw"] is not None:
                    deps.add(g["sw"])
                deps.update(g["sr"])
                g["subs"] = {}
                g["sw"] = op.h
                g["sr"] = []
            else:
                ent = g["subs"].get(sub)
                if ent is None:
                    ent = g["subs"][sub] = [None, []]
                if ent[0] is not None:
                    deps.add(ent[0])
                elif g["sw"] is not None:
                    deps.add(g["sw"])
                deps.update(ent[1])
                deps.update(g["sr"])
                ent[0] = op.h
                ent[1] = []
        deps.discard(op.h)
        op.deps = deps
        self.q[eng].append(op)
        return op

    def finalize(self):
        for e in self.ENGS:
            for op in self.q[e]:
                for d in op.deps:
                    if d[0] == "c":
                        if e == "pe" and d[1] == "pe":
                            continue
                        self.q[d[1]][d[2]].sig = True
        self.rank = {}
        for e in self.ENGS:
            r = 0
            rk = []
            for op in self.q[e]:
                if op.sig:
                    r += 1
                rk.append(r)
            self.rank[e] = rk

    def emit_engine(self, e, eo, sem_eng, sem_dma, final_waits=()):
        waited = {}
        for op in self.q[e]:
            need = {}
            for d in op.deps:
                if d[0] == "c":
                    if e == "pe" and d[1] == "pe":
                        continue
                    key = ("c", d[1])
                    val = self.rank[d[1]][d[2]]
                else:
                    key = ("d", d[1])
                    val = 16 * d[2]
                if need.get(key, 0) < val:
                    need[key] = val
            for key, val in need.items():
                if waited.get(key, 0) < val:
                    sem = sem_eng[key[1]] if key[0] == "c" else sem_dma[key[1]]
                    eo.wait_ge(sem, val)
                    waited[key] = val
            ins = op.fn(eo)
            if op.dma is not None:
                ins.then_inc(sem_dma[op.dma], 16)
            elif op.sig:
                ins.then_inc(sem_eng[e], 1)
        for key in final_waits:
            eo.wait_ge(sem_dma[key], 16 * self.dma_n[key])


def build_nc():
    nc = bass.Bass("TRN2", target_bir_lowering=False)
    dr = {}

    def din(name, shape):
        dr[name] = nc.dram_tensor(name, list(shape), F32, kind="ExternalInput").ap()
        return dr[name]

    xT_d = din("xT", [NSEQ, 4, 128, 2, 2048])
    xtok_d = din("xtok", [NSEQ, S, D])
    win_d = din("w_in_r", [20, 128, 1024])
    wout_d = din("w_out_r", [128, 8, 1024])
    wffi_d = din("w_ffi_r", [NJ, 128, 2048])
    wffo_d = din("w_ffo_r", [128, NJ, 1024])
    chan_d = din("chanvec", [128, 36])
    lwa_d = din("lru_w_a", [8, 64, 64])
    lwx_d = din("lru_w_x", [8, 64, 64])
    dal_d = din("da_lambda", [1, 256])
    dag_d = din("da_norm_g", [1, 128])
    ln_d = [din(n, [1, 1024]) for n in ("ln1_g", "ln1_b", "ln2_g", "ln2_b")]
    out_d = nc.dram_tensor("out", [NSEQ, S, D], F32, kind="ExternalOutput").ap()

    P = Prog()

    R1N = 72 * 256
    R2N = 54 * 256
    R3N = 30 * 256
    with (
        nc.sbuf_tensor("R1", [128, R1N], F32) as R1,
        nc.sbuf_tensor("R2", [128, R2N], F32) as R2,
        nc.sbuf_tensor("R3", [128, R3N], F32) as R3,
        nc.sbuf_tensor("mg", [128, 8 * S], BF16) as mg_t,
        nc.sbuf_tensor("lnp", [128, 4 * 1024], F32) as lnp_t,
        nc.sbuf_tensor("ident", [128, 128], BF16) as ident_t,
        nc.sbuf_tensor("ones", [128, 128], BF16) as ones_t,
        nc.sbuf_tensor("tri2", [128, 256], BF16) as tri_t,
        nc.sbuf_tensor("chan", [128, 84], F32) as chan_t,
        nc.sbuf_tensor("wbda", [128, 4 * 128], BF16) as wbda_t,
        nc.sbuf_tensor("wbdx", [128, 4 * 128], BF16) as wbdx_t,
        nc.sbuf_tensor("gtile", [128, 128], F32) as gt_t,
        nc.psum_tensor("ps", [128, 8 * 512], F32) as ps_t,
    ):
        def view(arena, off_b, shape, dt):
            esz = 2 if dt == BF16 else 4
            n = 1
            for s_ in shape[1:]:
                n *= s_
            nb = n * esz
            assert off_b % 4 == 0 and nb % 4 == 0
            ap = arena[:, off_b // 4:(off_b + nb) // 4]
            if dt == BF16:
                ap = ap.bitcast(BF16)
            if len(shape) == 3:
                ap = ap.rearrange("p (a b) -> p a b", a=shape[1])
            elif len(shape) == 4:
                ap = ap.rearrange("p (a b c) -> p a b c", a=shape[1], b=shape[2])
            return ap

        KB = 1024
        xT = view(R1, 0, [128, 4, 8, 512], BF16)
        win = view(R1, 32 * KB, [128, 20, 8, 128], BF16)
        wout = view(R1, 0, [128, 8, 1024], BF16)
        wffo = view(R1, 16 * KB, [128, NJ, 1024], BF16)
        wsb = view(R1, 60 * KB, [128, 3, 2048], BF16)
        XP = 2052
        o = 0
        xpad = view(R2, o, [128, XP], F32); o += XP * 4
        xc = view(R2, o, [128, S], F32); o += S * 4
        ra = view(R2, o, [128, S], F32); o += S * 4
        ib = view(R2, o, [128, S], F32); o += S * 4
        mb = view(R2, o, [128, S], F32); o += S * 4
        gg = view(R2, o, [128, S], F32); o += S * 4
        xcb = view(R2, o, [128, S], BF16); o += S * 2
        assert o <= R2N * 4
        qkT = view(R2, 0, [128, 8, S], BF16)
        vaug = view(R2, 32 * KB, [128, 16, 4, 130], BF16)
        assert 32 * KB + 16 * 4 * 130 * 2 <= R2N * 4
        x1 = view(R2, 0, [128, 2, 4, 1024], F32)
        hT = view(R2, 32 * KB, [128, NJ, 512], BF16)
        Eb = view(R3, 0, [128, 3, 2, 512], BF16)
        of_ = view(R3, 6 * KB, [128, 4, 128], F32)
        onb = view(R3, 8 * KB, [128, 4, 128], BF16)
        junk = view(R3, 9 * KB, [128, 128], F32)
        ybuf = view(R3, 0, [128, 2, 1024], F32)
        osb = view(R3, 8 * KB, [128, 2, 1024], F32)
        x1b2 = view(R3, 28 * KB, [128, 1024], BF16)
        x1b = view(R3, 16 * KB, [128, 1024], BF16)
        sg = view(R3, 18 * KB, [128, 2, 512], BF16)
        xt = view(R3, 20 * KB, [128, 2, 1024], F32)
        mg = mg_t[:, :].rearrange("p (k t) -> p k t", k=8)
        lnp = lnp_t[:, :].rearrange("p (a d) -> p a d", a=4)
        ident = ident_t[:, :]
        ones = ones_t[:, :]
        tri2 = tri_t[:, :].rearrange("p (c j) -> p c j", c=2)
        chan = chan_t[:, 0:36].rearrange("p (c v) -> p c v", c=4)
        cneg = chan_t[:, 36:40]
        wbda = wbda_t[:, :].rearrange("p (c e) -> p c e", c=4)
        wbdx = wbdx_t[:, :].rearrange("p (c e) -> p c e", c=4)
        gtile = gt_t[:, :]
        lamt = view(R3, 10 * KB, [128, 256], F32)
        sm = chan_t[:, 40:84]
        neglam = sm[:, 0:1]
        s1 = sm[:, 1:2]
        s2 = sm[:, 2:3]
        zt = sm[:, 8:16]
        ss = sm[:, 16:20]
        rstd4 = sm[:, 20:24]
        bst = sm[:, 24:36]
        mv = sm[:, 36:38]
        lrstd = sm[:, 38:39]
        nmr = sm[:, 39:40]
        sp4 = sm[:, 40:44]
        cneg2 = sm[:, 3:7]
        ps = ps_t[:, :].rearrange("p (b n) -> p b n", b=8)
        psb4 = ps_t[:, 4 * 512:5 * 512].bitcast(BF16)
        psb7 = ps_t[:, 7 * 512:8 * 512].bitcast(BF16)

        def MG(G, k, t):
            return (("mg", G), k, t)

        def PS(b):
            return ("ps", b) if b < 5 else ("pb%d" % b, "*")

        def mm(out, lhsT, rhs, start, stop, reads, writes, skip=False):
            P.add("pe", lambda e: e.matmul(out, lhsT=lhsT, rhs=rhs, start=start, stop=stop,
                                           skip_group_check=skip), reads, writes)

        def tr(out, in_, reads, writes):
            P.add("pe", lambda e: e.transpose(out, in_, ident), reads, writes)

        def act(out, in_, func, reads, writes, bias=None, scale=None, accum_out=None):
            kw = {}
            if bias is not None:
                kw["bias"] = bias
            if scale is not None:
                kw["scale"] = scale
            if accum_out is not None:
                kw["accum_out"] = accum_out
            P.add("act", lambda e: e.activation(out=out, in_=in_, func=func, **kw), reads, writes)

        def ts(eng, out, in0, s1_, s2_, op0, op1, reads, writes):
            if op1 is None:
                P.add(eng, lambda e: e.tensor_scalar(out=out, in0=in0, scalar1=s1_, scalar2=None, op0=op0),
                      reads, writes)
            else:
                P.add(eng, lambda e: e.tensor_scalar(out=out, in0=in0, scalar1=s1_, scalar2=s2_, op0=op0, op1=op1),
                      reads, writes)

        def stt(out, in0, scalar, in1, op0, op1, reads, writes):
            P.add("dve", lambda e: e.scalar_tensor_tensor(out=out, in0=in0, scalar=scalar, in1=in1, op0=op0, op1=op1),
                  reads, writes)

        def tt(eng, out, in0, in1, op, reads, writes):
            P.add(eng, lambda e: e.tensor_tensor(out=out, in0=in0, in1=in1, op=op), reads, writes)

        def cp(eng, out, in_, reads, writes):
            if eng == "act":
                P.add("act", lambda e: e.copy(out=out, in_=in_), reads, writes)
            else:
                P.add(eng, lambda e: e.tensor_copy(out=out, in_=in_), reads, writes)

        def dma(eng, out, in_, key, reads, writes):
            P.add(eng, lambda e: e.dma_start(out=out, in_=in_), reads, writes, dma=key)

        def memset(eng, ap, val, reads, writes):
            P.add(eng, lambda e: e.memset(ap, val), reads, writes)

        I32 = mybir.dt.int32

        def rsqrt_dve(y, v, tmp, tok_y, tok_v, tok_t):
            vi = v.bitcast(I32)
            yi = y.bitcast(I32)
            P.add("dve", lambda e: e.tensor_single_scalar(out=yi, in_=vi, scalar=1, op=ALU.logical_shift_right),
                  [tok_v], [tok_y])
            P.add("dve", lambda e: e.tensor_scalar(out=yi, in0=yi, scalar1=-1.0, scalar2=float(0x5f3759df),
                                                   op0=ALU.mult, op1=ALU.add), [tok_y], [tok_y])
            for _ in range(3):
                tt("dve", tmp, y, y, ALU.mult, [tok_y], [tok_t])
                tt("dve", tmp, tmp, v, ALU.mult, [tok_t, tok_v], [tok_t])
                ts("dve", tmp, tmp, -0.5, 1.5, ALU.mult, ALU.add, [tok_t], [tok_t])
                tt("dve", y, y, tmp, ALU.mult, [tok_y, tok_t], [tok_y])

        memset("dve", ones, 1.0, [], [("c", "ones")])
        P.add("pool", lambda e: e.affine_select(out=ident, in_=ones, pattern=[[1, 128]], compare_op=ALU.is_equal,
                                                fill=0.0, base=0, channel_multiplier=-1),
              [("c", "ones")], [("c", "ident")])
        memset("pool", tri2[:, 0, :], 0.0, [], [("c", "tri0")])
        P.add("pool", lambda e: e.affine_select(out=tri2[:, 1, :], in_=tri2[:, 0, :], pattern=[[1, 128]],
                                                compare_op=ALU.is_ge, fill=-30000.0, base=0,
                                                channel_multiplier=-1),
              [("c", "tri0")], [("c", "tri")])
        dma("sp", chan_t[:, 0:36], chan_d[:, :], ("su", 0), [], [("c", "chan")])
        dma("sp", lamt, dal_d[0:1, :].partition_broadcast(128), ("su", 1), [], [("R3", "lamt")])
        dma("sp", gtile, dag_d[0:1, :].partition_broadcast(128), ("su", 2), [], [("c", "gtile")])
        for i in range(4):
            dma("sp", lnp[:, i, :], ln_d[i][0:1, :].partition_broadcast(128), ("su", 3 + i), [], [("c", "lnp", i)])
        act(sp4, chan[:, :, 7], AF.Exp, [("c", "chan")], [("c", "sp4")], scale=-1.0)
        act(sp4, sp4, AF.Ln, [("c", "sp4")], [("c", "sp4")], bias=1.0)
        ts("dve", cneg, sp4, -4.0, None, ALU.mult, None, [("c", "sp4")], [("c", "cneg")])
        ts("dve", cneg2, sp4, -8.0, None, ALU.mult, None, [("c", "sp4")], [("c", "cneg")])
        ts("dve", chan[:, :, 5:7], chan[:, :, 5:7], 0.5, None, ALU.mult, None, [("c", "chan")], [("c", "chan")])
        ts("dve", gtile, gtile, 1.0 - LAMBDA_INIT, None, ALU.mult, None, [("c", "gtile")], [("c", "gtile")])
        tt("dve", junk[:, 0:64], lamt[:, 0:64], lamt[:, 64:128], ALU.mult, [("R3", "lamt")], [("R3", "junk")])
        P.add("dve", lambda e: e.reduce_sum(out=s1, in_=junk[:, 0:64], axis=AX.X), [("R3", "junk")], [("c", "s1")])
        tt("dve", junk[:, 64:128], lamt[:, 128:192], lamt[:, 192:256], ALU.mult, [("R3", "lamt")], [("R3", "junk2")])
        P.add("dve", lambda e: e.reduce_sum(out=s2, in_=junk[:, 64:128], axis=AX.X), [("R3", "junk2")], [("c", "s2")])
        act(s1, s1, AF.Exp, [("c", "s1")], [("c", "s1")])
        act(s2, s2, AF.Exp, [("c", "s2")], [("c", "s2")])
        tt("dve", neglam, s2, s1, ALU.subtract, [("c", "s1"), ("c", "s2")], [("c", "neglam")])
        ts("dve", neglam, neglam, -LAMBDA_INIT, None, ALU.add, None, [("c", "neglam")], [("c", "neglam")])

        wst = view(R3, 20 * KB, [128, 2, 4, 128], F32)

        def load_wbd():
            memset("dve", wst, 0.0, [], [("R3", "wst", gi, n) for gi in range(2) for n in range(8)])
            for gi, src in enumerate((lwa_d, lwx_d)):
                for n in range(8):
                    cc, hf = n // 2, n % 2
                    dma("sp", wst[hf * 64:(hf + 1) * 64, gi, cc, hf * 64:(hf + 1) * 64], src[n, :, :],
                        ("su", 7 + 8 * gi + n), [], [("R3", "wst", gi, n)])

        def cast_wbd():
            cp("dve", wbda, wst[:, 0, :, :], [("R3", "wst", 0, n) for n in range(8)], [("c", "wbda")])
            cp("dve", wbdx, wst[:, 1, :, :], [("R3", "wst", 1, n) for n in range(8)], [("c", "wbdx")])

        rot = {"a2": 0, "a1": 0, "ev": 0}

        def load_win(blk):
            dma("pool", win[:, blk, :, :].rearrange("p k c -> p (k c)"), win_d[blk], ("win", blk), [],
                [("R1", "win", blk)])

        def load_xT(s, tg, star):
            tsl = slice(tg * 512, (tg + 1) * 512)
            w = [("R1", "*")] if star else [("R1", "xT", tg)]
            dma("pool", xT[:, tg, :, :].rearrange("p (a k) t -> p a (k t)", a=2), xT_d[s, tg, :, :, :],
                ("xT", tg), [], w)

        def load_seq_inputs(s):
            load_xT(s, 0, s > 0)
            load_win(12)
            load_win(16)
            for tg in range(1, 4):
                load_xT(s, tg, False)
            for cc in range(1, 4):
                load_win(12 + cc)
                load_win(16 + cc)
            for blk in range(12):
                load_win(blk)

        def phase_lru(s):
            gg2 = view(R3, 0, [128, 2, S], F32)
            sqb = view(R3, 16 * KB, [128, S], BF16)
            r3_first = [True]
            TS = [slice(tg * 512, (tg + 1) * 512) for tg in range(4)]
            memset("dve", xpad[:, 0:3], 0.0, [], [("R2", "*")])

            def proj(blk, tg, b):
                for k in range(8):
                    mm(ps[:, b, :], win[:, blk, k, :], xT[:, tg, k, :], k == 0, k == 7,
                       [("R1", "win", blk), ("R1", "xT", tg)], [PS(b)])

            PB = [0, 1, 4, 5, 6]

            def head(cc):
                par = cc % 2
                for tg in range(4):
                    b = PB[rot["a2"] % 5]
                    rot["a2"] += 1
                    proj(12 + cc, tg, b)
                    cp("act", xpad[:, 3 + tg * 512:3 + (tg + 1) * 512], ps[:, b, :], [PS(b)], [("R2", "xpad", tg)])
                for tg in range(4):
                    b = PB[rot["a2"] % 5]
                    rot["a2"] += 1
                    proj(16 + cc, tg, b)
                    w = [("R3", "*")] if r3_first[0] else [("R3", "gg", par, tg)]
                    r3_first[0] = False
                    act(gg2[:, par, TS[tg]], ps[:, b, :], AF.Gelu_apprx_tanh, [PS(b)], w)

            def conv_gates(cc, tg):
                tsl = TS[tg]
                xr = [("R2", "xpad", tg)] + ([("R2", "xpad", tg - 1)] if tg > 0 else [])
                act(xc[:, tsl], xpad[:, 3 + tg * 512:3 + (tg + 1) * 512], AF.Identity, xr + [("c", "chan")],
                    [("R2", "xc", tg)], bias=chan[:, cc, 4:5], scale=chan[:, cc, 3:4])
                for tap in (2, 1, 0):
                    stt(xc[:, tsl], xpad[:, tap + tg * 512:tap + (tg + 1) * 512], chan[:, cc, tap:tap + 1],
                        xc[:, tsl], ALU.mult, ALU.add, xr + [("R2", "xc", tg)], [("R2", "xc", tg)])
                cp("act", xcb[:, tsl], xc[:, tsl], [("R2", "xc", tg)], [("R2", "xcb", tg)])
                mm(ps[:, 2, :], wbda[:, cc, :], xcb[:, tsl], True, True, [("c", "wbda"), ("R2", "xcb", tg)], [PS(2)])
                act(ra[:, tsl], ps[:, 2, :], AF.Tanh, [PS(2), ("c", "chan")], [("R2", "ra", tg)],
                    bias=chan[:, cc, 5:6], scale=0.5)
                mm(ps[:, 3, :], wbdx[:, cc, :], xcb[:, tsl], True, True, [("c", "wbdx"), ("R2", "xcb", tg)], [PS(3)])
                act(ib[:, tsl], ps[:, 3, :], AF.Tanh, [PS(3), ("c", "chan")], [("R2", "ib", tg)],
                    bias=chan[:, cc, 6:7], scale=0.5)
                act(mb[:, tsl], ra[:, tsl], AF.Exp, [("R2", "ra", tg), ("c", "cneg")], [("R2", "mb", tg)],
                    bias=cneg2[:, cc:cc + 1], scale=cneg2[:, cc:cc + 1])
                act(ra[:, tsl], ra[:, tsl], AF.Exp, [("R2", "ra", tg), ("c", "cneg")], [("R2", "ra", tg)],
                    bias=cneg[:, cc:cc + 1], scale=cneg[:, cc:cc + 1])

            def tail_a(cc):
                for tg in range(4):
                    tsl = TS[tg]
                    act(mb[:, tsl], mb[:, tsl], AF.Sqrt, [("R2", "mb", tg)], [("R2", "mb", tg)], bias=1.0, scale=-1.0)
                memset("dve", mb[:, 0:1], 1.0, [("R2", "mb", 0)], [("R2", "mb", 0)])

            carry = sm[:, 40:41]

            def tail_tg(cc, tg):
                par = cc % 2
                tsl = TS[tg]
                stt(ib[:, tsl], ib[:, tsl], 1.0, xc[:, tsl], ALU.add, ALU.mult,
                    [("R2", "ib", tg), ("R2", "xc", tg)], [("R2", "ib", tg)])
                stt(ib[:, tsl], ib[:, tsl], 0.5, mb[:, tsl], ALU.mult, ALU.mult,
                    [("R2", "ib", tg), ("R2", "mb", tg)], [("R2", "ib", tg)])
                if tg == 0:
                    P.add("dve", lambda e: e.tensor_tensor_scan(
                        out=xc[:, tsl], data0=ra[:, tsl], data1=ib[:, tsl], initial=0.0,
                        op0=ALU.mult, op1=ALU.add),
                        [("R2", "ra", tg), ("R2", "ib", tg), ("R2", "xc", tg)], [("R2", "xc", tg)])
                else:
                    P.add("dve", lambda e: e.tensor_tensor_scan(
                        out=xc[:, tsl], data0=ra[:, tsl], data1=ib[:, tsl], initial=carry,
                        op0=ALU.mult, op1=ALU.add),
                        [("R2", "ra", tg), ("R2", "ib", tg), ("R2", "xc", tg), ("c", "carry")], [("R2", "xc", tg)])
                if tg < 3:
                    cp("dve", carry, xc[:, (tg + 1) * 512 - 1:(tg + 1) * 512], [("R2", "xc", tg)], [("c", "carry")])
                mgw = [MG(tg, 4 + cc, t) for t in range(4)]
                tt("dve", mg[:, 4 + cc, tsl], xc[:, tsl], gg2[:, par, tsl], ALU.mult,
                   [("R2", "xc", tg), ("R3", "gg", par, tg)], mgw)

            head(0)
            if s == 0:
                cast_wbd()
            for tg in range(4):
                conv_gates(0, tg)
            for cc in range(4):
                tail_a(cc)
                if cc + 1 < 4:
                    head(cc + 1)
                for tg in range(4):
                    tail_tg(cc, tg)
                    if cc + 1 < 4:
                        conv_gates(cc + 1, tg)
            rs = gg2[:, 0, :]
            units = []

            def u_sq(tg, cc):
                def f():
                    tsl = TS[tg]
                    mgw = [MG(tg, 4 + cc, t) for t in range(4)]
                    act(sqb[:, TS[cc]], mg[:, 4 + cc, tsl], AF.Square, mgw, [("R3", "sq", cc)])
                    mm(ps[:, 4 + tg, :], ones, sqb[:, TS[cc]], cc == 0, cc == 3, [("c", "ones"), ("R3", "sq", cc)],
                       [PS(4 + tg)])
                return f

            def u_rs(tg):
                def f():
                    tsl = TS[tg]
                    act(rs[:, tsl], ps[:, 4 + tg, :], AF.Sqrt, [PS(4 + tg)], [("R3", "gg", 0, tg)],
                        bias=RMS_EPS, scale=1.0 / 512.0)
                    P.add("dve", lambda e: e.reciprocal(out=rs[:, tsl], in_=rs[:, tsl]),
                          [("R3", "gg", 0, tg)], [("R3", "gg", 0, tg)])
                return f

            def u_scale(tg, cc):
                def f():
                    tsl = TS[tg]
                    mgw = [MG(tg, 4 + cc, t) for t in range(4)]
                    stt(mg[:, 4 + cc, tsl], mg[:, 4 + cc, tsl], chan[:, cc, 8:9], rs[:, tsl], ALU.mult, ALU.mult,
                        mgw + [("R3", "gg", 0, tg), ("c", "chan")], mgw)
                return f

            for tg in range(4):
                for cc in range(4):
                    units.append(u_sq(tg, cc))
            for tg in range(4):
                units.append(u_rs(tg))
                for cc in range(4):
                    units.append(u_scale(tg, cc))
            return units

        def phase_qkv(s, units):
            memset("dve", vaug[:, :, :, 128:129], 1.0, [], [("R2", "*")])
            for ch in range(8):
                for tg in range(4):
                    tsl = slice(tg * 512, (tg + 1) * 512)
                    b = rot["a1"] % 4
                    rot["a1"] += 1
                    for k in range(8):
                        mm(ps[:, b, :], win[:, ch, k, :], xT[:, tg, k, :], k == 0, k == 7,
                           [("R1", "win", ch), ("R1", "xT", tg)], [PS(b)])
                    eng = "act" if rot["ev"] % 2 == 0 else "dve"
                    rot["ev"] += 1
                    cp(eng, qkT[:, ch, tsl], ps[:, b, :], [PS(b)], [("R2", "qk", ch, tg)])
                    if units:
                        units.pop(0)()
            for t16 in range(16):
                b = rot["a1"] % 4
                rot["a1"] += 1
                for k in range(8):
                    mm(ps[:, b, :].rearrange("p (h e) -> p h e", h=4),
                       xT[:, t16 // 4, k, (t16 % 4) * 128:(t16 % 4 + 1) * 128],
                       win[:, 8:12, k, :], k == 0, k == 7,
                       [("R1", "win", 8 + i) for i in range(4)] + [("R1", "xT", t16 // 4)], [PS(b)])
                eng = "act" if rot["ev"] % 2 == 0 else "dve"
                rot["ev"] += 1
                cp(eng, vaug[:, t16, :, 0:128], ps[:, b, :].rearrange("p (h e) -> p h e", h=4), [PS(b)],
                   [("R2", "v", t16)])
                if units:
                    units.pop(0)()
            while units:
                units.pop(0)()

        def load_cd_weights(s):
            for k in range(8):
                w = [("R1", "*")] if k == 0 else [("R1", "wout", k)]
                dma("pool", wout[:, k, :], wout_d[:, k, :], ("wout", k), [], w)
            for jj in range(NJ // 2):
                dma("pool", wffo[:, 2 * jj:2 * jj + 2, :], wffo_d[:, 2 * jj:2 * jj + 2, :], ("wffo", jj), [],
                    [("R1", "wffo", jj)])

        def acc_ap(a, lo, hi):
            b = 5 + a // 3
            off = (a % 3) * 129
            return ps[:, b, off + lo:off + hi]

        def acc_tok(a):
            return ("pb%d" % (5 + a // 3), a)

        def phase_attn(s):
            first_e = [True]
            steps = []
            for h in range(4):
                for qg in range(4):
                    nfull = 4 * qg
                    lst = [(kb, -1) for kb in range(nfull)] + [(nfull + i, i) for i in range(4)]
                    for si, (kb, i) in enumerate(lst):
                        steps.append((h, qg, kb, i, si == 0, si == len(lst) - 1))
            ecnt = [0]

            def emit_qk(n):
                h, qg, kb, i, _, _ = steps[n]
                sp_ = n % 2
                col0 = 0 if i < 0 else i * 128
                qtok = [("R2", "qk", 4 + h, kb // 4), ("R2", "qk", h, qg)]
                for c in range(2):
                    kT_c = qkT[c * 64:(c + 1) * 64, 4 + h, kb * 128:(kb + 1) * 128]
                    bank = 2 * sp_ + c
                    if i < 0:
                        mm(ps[:, bank, 0:512], kT_c, qkT[c * 64:(c + 1) * 64, h, qg * 512:(qg + 1) * 512],
                           True, True, qtok, [("ps", bank)])
                    else:
                        q0 = qg * 512 + col0
                        mm(ps[:, bank, col0:col0 + 128], kT_c, qkT[c * 64:(c + 1) * 64, h, q0:q0 + 128],
                           True, False, qtok, [("ps", bank)])
                        mm(ps[:, bank, col0:col0 + 128], ident, tri2[:, 1, :], False, True,
                           [("c", "ident"), ("c", "tri")], [("ps", bank)])
                        if col0 + 128 < 512:
                            mm(ps[:, bank, col0 + 128:512], kT_c,
                               qkT[c * 64:(c + 1) * 64, h, q0 + 128:(qg + 1) * 512],
                               True, True, qtok, [("ps", bank)])

            def emit_exp(n):
                h, qg, kb, i, _, _ = steps[n]
                sp_ = n % 2
                eb = n % 3
                col0 = 0 if i < 0 else i * 128
                w = [("R3", "*")] if first_e[0] else [("R3", "E", eb)]
                first_e[0] = False
                act(Eb[:, eb, :, col0:512], ps[:, 2 * sp_:2 * sp_ + 2, col0:512], AF.Exp,
                    [("ps", 2 * sp_), ("ps", 2 * sp_ + 1)], w, scale=0.125)

            def emit_pv(n):
                h, qg, kb, i, is_first, is_last = steps[n]
                eb = n % 3
                qb0 = 0 if i < 0 else i
                for qb in range(qb0, 4):
                    for c in range(2):
                        a = qb * 2 + c
                        st = is_first and (a % 3 == 0)
                        sp = (kb == 4 * qg + qb)
                        mm(acc_ap(a, 0, 129), Eb[:, eb, c, qb * 128:(qb + 1) * 128], vaug[:, kb, h, 0:129],
                           st, sp, [("R3", "E", eb), ("R2", "v", kb)], [acc_tok(a)], skip=True)
                if is_last:
                    finalize(h, qg, n)
                while pend and pend[0][0] <= n:
                    fin_b(*pend.pop(0)[1])

            accsb = view(R3, 10 * KB, [128, 3, 387], F32)
            onb2 = view(R3, 16 * KB, [128, 2, 4, 128], BF16)
            pend = []
            gcnt = [0]

            def finalize(h, qg, n):
                while len(pend) > 1:
                    fin_b(*pend.pop(0)[1])
                gp = gcnt[0] % 2
                gcnt[0] += 1
                for bi in range(3):
                    na = 3 if bi < 2 else 2
                    cp("dve", accsb[:, bi, 0:na * 129], ps[:, 5 + bi, 0:na * 129],
                       [acc_tok(3 * bi + j) for j in range(na)], [("R3", "accsb", bi)])
                av = accsb.rearrange("p b (a c) -> p b a c", c=129)
                zt4 = zt.rearrange("p (a o) -> p a o", o=1)
                P.add("dve", lambda e: e.reciprocal(out=zt[:, 0:6].rearrange("p (b a o) -> p b a o", b=2, o=1),
                                                    in_=av[:, 0:2, :, 128:129]),
                      [("R3", "accsb", 0), ("R3", "accsb", 1)], [("c", "zt")])
                P.add("dve", lambda e: e.reciprocal(out=zt4[:, 6:8, :], in_=av[:, 2, 0:2, 128:129]),
                      [("R3", "accsb", 2)], [("c", "zt")])
                ztv = zt.rearrange("p (q c) -> p q c", c=2)
                ts("dve", ztv[:, :, 1:2], ztv[:, :, 1:2], neglam, None, ALU.mult, None,
                   [("c", "zt"), ("c", "neglam")], [("c", "zt")])

                def acs(a):
                    return accsb[:, a // 3, (a % 3) * 129:(a % 3) * 129 + 128]

                for qb in range(4):
                    rd = [("R3", "accsb", (2 * qb) // 3), ("R3", "accsb", (2 * qb + 1) // 3), ("c", "zt")]
                    ts("dve", of_[:, qb, :], acs(2 * qb), zt[:, 2 * qb:2 * qb + 1], None, ALU.mult, None,
                       rd, [("R3", "of", qb)])
                    stt(of_[:, qb, :], acs(2 * qb + 1), zt[:, 2 * qb + 1:2 * qb + 2], of_[:, qb, :],
                        ALU.mult, ALU.add, rd + [("R3", "of", qb)], [("R3", "of", qb)])
                    P.add("dve", lambda e, qb=qb: e.scalar_tensor_tensor(
                        out=junk, in0=of_[:, qb, :], scalar=1.0, in1=of_[:, qb, :], op0=ALU.mult, op1=ALU.mult,
                        accum_out=ss[:, qb:qb + 1]), [("R3", "of", qb)], [("R3", "junk"), ("c", "ss")])
                ts("dve", ss, ss, 1.0 / 128.0, RMS_EPS, ALU.mult, ALU.add, [("c", "ss")], [("c", "ss")])
                rsqrt_dve(rstd4, ss, zt[:, 0:4], ("c", "rstd4"), ("c", "ss"), ("c", "zt"))
                for qb in range(4):
                    stt(onb2[:, gp, qb, :], of_[:, qb, :], rstd4[:, qb:qb + 1], gtile, ALU.mult, ALU.mult,
                        [("R3", "of", qb), ("c", "rstd4"), ("c", "gtile")], [("R3", "onb", gp, qb)])
                pend.append((n + 14, (h, qg, gp)))

            def fin_b(h, qg, gp):
                for qb in range(4):
                    tr(psb4[:, qb * 128:(qb + 1) * 128], onb2[:, gp, qb, :],
                       [("R3", "onb", gp, qb), ("c", "ident")], [PS(4)])
                cp("dve", mg[:, h, qg * 512:(qg + 1) * 512], psb4[:, 0:512], [PS(4)],
                   [MG(qg, h, t) for t in range(4)])

            N = len(steps)
            emit_qk(0)
            for n in range(N):
                emit_exp(n)
                if n + 1 < N:
                    emit_qk(n + 1)
                emit_pv(n)
            while pend:
                fin_b(*pend.pop(0)[1])

        def lnsc(p):
            if p == 0:
                return dict(bst=sm[:, 24:36], mv=sm[:, 36:38], lr=sm[:, 38:39], nm=sm[:, 39:40], tmp=sm[:, 1:2],
                            t_bst=[("c", "bst0")], t_mv=("c", "mv0"), t_lr=("c", "lr0"), t_nm=("c", "nm0"),
                            t_tmp=("c", "tmp0"))
            return dict(bst=sm[:, 8:20], mv=sm[:, 20:22], lr=sm[:, 22:23], nm=sm[:, 23:24], tmp=sm[:, 7:8],
                        t_bst=[("c", "zt"), ("c", "ss")], t_mv=("c", "rstd4"), t_lr=("c", "rstd4"),
                        t_nm=("c", "rstd4"), t_tmp=("c", "tmp1"))

        def rsqrt_col(y, v, tmp, tok_y, tok_v, tok_t):
            vi = v.bitcast(I32)
            yi = y.bitcast(I32)
            P.add("dve", lambda e: e.tensor_single_scalar(out=yi, in_=vi, scalar=1, op=ALU.logical_shift_right),
                  [tok_v], [tok_y])
            P.add("dve", lambda e: e.tensor_scalar(out=yi, in0=yi, scalar1=-1.0, scalar2=float(0x5f3759df),
                                                   op0=ALU.mult, op1=ALU.add), [tok_y], [tok_y])
            for _ in range(3):
                stt(tmp, y, v, y, ALU.mult, ALU.mult, [tok_y, tok_v], [tok_t])
                ts("dve", tmp, tmp, -0.5, 1.5, ALU.mult, ALU.add, [tok_t], [tok_t])
                tt("dve", y, y, tmp, ALU.mult, [tok_y, tok_t], [tok_y])

        def ln_stats(yb, y_tok, sc):
            for c in range(2):
                P.add("dve", lambda e, c=c: e.bn_stats(out=sc["bst"][:, 6 * c:6 * c + 6],
                                                       in_=yb[:, c * 512:(c + 1) * 512]),
                      [y_tok], sc["t_bst"])
            P.add("dve", lambda e: e.bn_aggr(out=sc["mv"], in_=sc["bst"]), sc["t_bst"], [sc["t_mv"]])
            ts("dve", sc["mv"][:, 1:2], sc["mv"][:, 1:2], LN_EPS, None, ALU.add, None, [sc["t_mv"]], [sc["t_mv"]])

        def ln_rstd(sc):
            rsqrt_col(sc["lr"], sc["mv"][:, 1:2], sc["tmp"], sc["t_lr"], sc["t_mv"], sc["t_tmp"])
            stt(sc["nm"], sc["mv"][:, 0:1], -1.0, sc["lr"], ALU.mult, ALU.mult, [sc["t_mv"], sc["t_lr"]],
                [sc["t_nm"]])

        def ln_norm(yb, y_tok, sc):
            act(yb, yb, AF.Identity, [y_tok, sc["t_lr"], sc["t_nm"]], [y_tok], bias=sc["nm"], scale=sc["lr"])

        def ln_affine(yb, y_tok, gi, bi_, dst, dst_tok, eng="dve"):
            tt(eng, dst, yb, lnp[:, gi, :], ALU.mult, [y_tok, ("c", "lnp", gi)], [dst_tok])
            tt(eng, dst, dst, lnp[:, bi_, :], ALU.add, [dst_tok, ("c", "lnp", bi_)], [dst_tok])

        def layer_norm(yb, gi, bi_, dst, y_tok, dst_tok, sc):
            ln_stats(yb, y_tok, sc)
            ln_rstd(sc)
            ln_norm(yb, y_tok, sc)
            ln_affine(yb, y_tok, gi, bi_, dst, dst_tok)

        cnt = {"xt": 0, "y": 0, "o": 0, "ws": 0, "gu": 0, "sg": 0, "acc": 0}

        def phase_cd(s):
            st = {"r2_first": True, "r3_first": True}

            ctxs = {}

            def c_stage(G, t, d):
                t16 = 4 * G + t
                par = G % 2
                if d == 0:
                    xs = cnt["xt"] % 2
                    cnt["xt"] += 1
                    w = [("R3", "*")] if st["r3_first"] else [("R3", "xt", xs)]
                    st["r3_first"] = False
                    dma("sp", xt[:, xs, :], xtok_d[s, t16 * 128:(t16 + 1) * 128, :], ("xt", xs), [], w)
                    ab = 2 * (cnt["acc"] % 2)
                    cnt["acc"] += 1
                    for hf in range(2):
                        for k in range(8):
                            mm(ps[:, ab + hf, :], mg[:, k, t16 * 128:(t16 + 1) * 128],
                               wout[:, k, hf * 512:(hf + 1) * 512], k == 0, k == 7,
                               [MG(G, k, t), ("R1", "wout", k)], [PS(ab + hf)])
                    ctxs[(G, t)] = dict(xs=xs, ab=ab, ys=t % 2, sc=lnsc(t % 2))
                    return
                c = ctxs[(G, t)]
                ys = c["ys"]
                yb = ybuf[:, ys, :]
                ytok = ("R3", "y", ys)
                if d == 1:
                    ab = c["ab"]
                    stt(yb, xt[:, c["xs"], :], ALPHA, ps_t[:, ab * 512:(ab + 2) * 512], ALU.mult, ALU.add,
                        [("R3", "xt", c["xs"]), PS(ab), PS(ab + 1)], [ytok])
                    ln_stats(yb, ytok, c["sc"])
                elif d == 2:
                    ln_rstd(c["sc"])
                elif d == 3:
                    ln_norm(yb, ytok, c["sc"])
                elif d == 4:
                    dst_tok = ("R2", "*") if st["r2_first"] else ("R2", "x1", par, t)
                    st["r2_first"] = False
                    ln_affine(yb, ytok, 0, 1, x1[:, par, t, :], dst_tok, eng="pool" if G == 0 else "dve")
                elif d == 5:
                    xb_ = x1b if t % 2 == 0 else x1b2
                    cp("act", xb_, x1[:, par, t, :], [("R2", "x1", par, t)], [("R3", "x1b", t % 2)])
                elif d == 6:
                    xb_ = x1b if t % 2 == 0 else x1b2
                    for k in range(8):
                        tr(psb7[:, k * 128:(k + 1) * 128], xb_[:, k * 128:(k + 1) * 128],
                           [("R3", "x1b", t % 2), ("c", "ident")], [PS(7)])
                    cp("act" if G == 0 else "dve", mg[:, :, t16 * 128:(t16 + 1) * 128],
                       psb7.rearrange("p (k t) -> p k t", k=8), [PS(7)], [MG(G, k, t) for k in range(8)])

            for v in range(0, 6 + 2 * 3 + 1):
                for t in range(4):
                    d = v - 2 * t
                    if 0 <= d <= 6:
                        c_stage(0, t, d)
            ln2_def = {}
            for G in range(4):
                par = G % 2
                x1T_r = [(("mg", G), "*")]
                for j in range(NJ):
                    wsl = cnt["ws"] % 3
                    cnt["ws"] += 1
                    dma("pool", wsb[:, wsl, :], wffi_d[j, :, :], ("ws", wsl), [], [("R1", "ws", wsl)])
                    wv = wsb[:, wsl, :].rearrange("p (k c) -> p k c", k=8)
                    gbk = []
                    for u in range(2):
                        gb = 4 + (cnt["gu"] % 3)
                        cnt["gu"] += 1
                        gbk.append(gb)
                        for k in range(8):
                            mm(ps[:, gb, :], wv[:, k, u * 128:(u + 1) * 128], mg[:, k, G * 512:(G + 1) * 512],
                               k == 0, k == 7, [("R1", "ws", wsl)] + x1T_r, [PS(gb)])
                    sgs = cnt["sg"] % 2
                    cnt["sg"] += 1
                    act(sg[:, sgs, :], ps[:, gbk[0], :], AF.Silu, [PS(gbk[0])], [("R3", "sg", sgs)])
                    tt("dve", hT[:, j, :], sg[:, sgs, :], ps[:, gbk[1], :], ALU.mult,
                       [("R3", "sg", sgs), PS(gbk[1])], [("R2", "hT", j)])
                    for f_ in ln2_def.pop(j, []):
                        f_()
                    if G + 1 < 4:
                        for t in range(4):
                            d = j - 5 * t
                            if 0 <= d <= 6:
                                c_stage(G + 1, t, d)
                for t in range(4):
                    t16 = 4 * G + t
                    ab = 2 * (cnt["acc"] % 2)
                    cnt["acc"] += 1
                    for hf in range(2):
                        for j in range(NJ):
                            mm(ps[:, ab + hf, :], hT[:, j, t * 128:(t + 1) * 128],
                               wffo[:, j, hf * 512:(hf + 1) * 512], j == 0, j == NJ - 1,
                               [("R2", "hT", j), ("R1", "wffo", j // 2)], [PS(ab + hf)])
                    ys = t % 2
                    os_ = cnt["o"] % 2
                    cnt["o"] += 1
                    sc_ = lnsc(t % 2)

                    def l0(ys=ys, ab=ab, par=par, t=t, sc_=sc_):
                        stt(ybuf[:, ys, :], x1[:, par, t, :], ALPHA, ps_t[:, ab * 512:(ab + 2) * 512],
                            ALU.mult, ALU.add, [("R2", "x1", par, t), PS(ab), PS(ab + 1)], [("R3", "y", ys)])
                        ln_stats(ybuf[:, ys, :], ("R3", "y", ys), sc_)

                    def l1(sc_=sc_):
                        ln_rstd(sc_)

                    def l2(ys=ys, sc_=sc_):
                        ln_norm(ybuf[:, ys, :], ("R3", "y", ys), sc_)

                    def l3(ys=ys, os_=os_, t16=t16):
                        ln_affine(ybuf[:, ys, :], ("R3", "y", ys), 2, 3, osb[:, os_, :], ("R3", "o", os_))
                        dma("sp", out_d[s, t16 * 128:(t16 + 1) * 128, :], osb[:, os_, :], ("st", os_),
                            [("R3", "o", os_)], [])

                    if t == 3 and G < 3:
                        ln2_def[0] = [l0]
                        ln2_def[1] = [l1]
                        ln2_def[2] = [l2]
                        ln2_def[3] = [l3]
                    else:
                        l0()
                        l1()
                        l2()
                        l3()

        for s in range(NSEQ):
            if s == 0:
                load_wbd()
            load_seq_inputs(s)
            units = phase_lru(s)
            phase_qkv(s, units)
            load_cd_weights(s)
            phase_attn(s)
            phase_cd(s)

        P.finalize()
        keys = sorted(P.dma_n.keys(), key=str)
        sem_eng = {}
        sem_dma = {}
        for e in ("pe", "act", "dve", "pool"):
            sem_eng[e] = nc.alloc_semaphore(name="se_" + e)
        for i, k in enumerate(keys):
            sem_dma[k] = nc.alloc_semaphore(name="sd_%d" % i)
        with nc.Block() as block:
            @block.tensor
            def _(e):
                P.emit_engine("pe", e, sem_eng, sem_dma)

            @block.scalar
            def _(e):
                P.emit_engine("act", e, sem_eng, sem_dma)

            @block.vector
            def _(e):
                P.emit_engine("dve", e, sem_eng, sem_dma)

            @block.gpsimd
            def _(e):
                P.emit_engine("pool", e, sem_eng, sem_dma)

            @block.sync
            def _(e):
                P.emit_engine("sp", e, sem_eng, sem_dma, final_waits=[("st", 0), ("st", 1)])
    return nc


def _prep_shared(inp):
    f = lambda a: np.ascontiguousarray(np.asarray(a, dtype=np.float32))
    w_in = f(inp["w_in"])[0]
    w_out = f(inp["w_out"])[0]
    w_ffi = f(inp["w_ffn_in"])[0]
    w_ffo = f(inp["w_ffn_out"])[0]
    sh = {}
    sh["w_in_r"] = f(w_in.reshape(8, 128, 20, 128).transpose(2, 1, 0, 3).reshape(20, 128, 1024))
    sh["w_out_r"] = f(w_out.reshape(8, 128, 1024).transpose(1, 0, 2))
    g = w_ffi[:, :DFF].reshape(8, 128, NJ, 128)
    u = w_ffi[:, DFF:].reshape(8, 128, NJ, 128)
    gu = np.concatenate([g, u], axis=3)
    sh["w_ffi_r"] = f(gu.transpose(2, 1, 0, 3).reshape(NJ, 128, 2048))
    sh["w_ffo_r"] = f(w_ffo.reshape(NJ, 128, 1024).transpose(1, 0, 2))
    cv = np.zeros((128, 4, 9), np.float32)
    conv_w = f(inp["conv_w"])[0]
    for tap in range(4):
        cv[:, :, tap] = conv_w[tap].reshape(4, 128).T
    for i, n in enumerate(["conv_b", "lru_b_a", "lru_b_x", "lru_lambda", "rec_norm_g"]):
        cv[:, :, 4 + i] = f(inp[n])[0].reshape(4, 128).T
    sh["chanvec"] = f(cv.reshape(128, 36))
    sh["lru_w_a"] = f(inp["lru_w_a"])[0]
    sh["lru_w_x"] = f(inp["lru_w_x"])[0]
    sh["da_lambda"] = f(inp["da_lambda"])[0].reshape(1, 256)
    sh["da_norm_g"] = f(inp["da_norm_g"])[0].reshape(1, 128)
    for n in ("ln1_g", "ln1_b", "ln2_g", "ln2_b"):
        sh[n] = f(inp[n])[0].reshape(1, 1024)
    return sh


def kernel(**inputs):
    x = np.asarray(inputs["x"], dtype=np.float32)
    sh = _prep_shared(inputs)
    in_maps = []
    for c in range(NCORES):
        xc = x[c * NSEQ:(c + 1) * NSEQ]
        xT = np.ascontiguousarray(xc.reshape(NSEQ, 4, 512, 8, 128).transpose(0, 1, 4, 3, 2)).reshape(
            NSEQ, 4, 128, 2, 2048)
        m = dict(sh)
        m["xT"] = xT
        m["xtok"] = np.ascontiguousarray(xc)
        in_maps.append(m)
    nc = build_nc()
    res = run_bass_kernel_spmd(nc, in_maps, core_ids=list(range(NCORES)))
    out = np.concatenate([np.asarray(r["out"], dtype=np.float32) for r in res.results], axis=0)
    return out


if __name__ == "__main__":
    import time
    t0 = time.time()
    nc = build_nc()
    print("build ok", time.time() - t0)
```

```python
import numpy as np
import concourse.bass as bass
import concourse.mybir as mybir
from concourse.bass_utils import run_bass_kernel_spmd

F32 = mybir.dt.float32
BF16 = mybir.dt.bfloat16
AF = mybir.ActivationFunctionType
ALU = mybir.AluOpType
AX = mybir.AxisListType

NCORES = 8
NSEQ = 2
S = 2048
D = 1024
DFF = 2816
NJ = DFF // 128
ALPHA = float(2.0 ** 0.25)
LAMBDA_INIT = 0.2
LN_EPS = 1e-5
RMS_EPS = 1e-5


class _Op:
    __slots__ = ("eng", "fn", "idx", "dma", "h", "deps", "sig")


class Prog:
    ENGS = ("pe", "act", "dve", "pool", "sp")

    def __init__(self):
        self.q = {e: [] for e in self.ENGS}
        self.tok = {}
        self.dma_n = {}

    def _grp(self, g):
        return self.tok.setdefault(g, {"sw": None, "sr": [], "subs": {}})

    def add(self, eng, fn, reads=(), writes=(), dma=None):
        op = _Op()
        op.eng = eng
        op.fn = fn
        op.idx = len(self.q[eng])
        op.dma = dma
        op.sig = False
        if dma is not None:
            n = self.dma_n.get(dma, 0) + 1
            self.dma_n[dma] = n
            op.h = ("d", dma, n)
        else:
            op.h = ("c", eng, op.idx)
        deps = set()
        for t in reads:
            g = self._grp(t[0])
            sub = tuple(t[1:])
            if sub == ("*",):
                for ent in g["subs"].values():
                    if ent[0] is not None:
                        deps.add(ent[0])
                if g["sw"] is not None:
                    deps.add(g["sw"])
                g["sr"].append(op.h)
            else:
                ent = g["subs"].get(sub)
                if ent is None:
                    ent = g["subs"][sub] = [None, []]
                if ent[0] is not None:
                    deps.add(ent[0])
                elif g["sw"] is not None:
                    deps.add(g["sw"])
                ent[1].append(op.h)
        for t in writes:
            g = self._grp(t[0])
            sub = tuple(t[1:])
            if sub == ("*",):
                for ent in g["subs"].values():
                    if ent[0] is not None:
                        deps.add(ent[0])
                    deps.update(ent[1])
                if g["sw"] is not None:
                    deps.add(g["sw"])
                deps.update(g["sr"])
                g["subs"] = {}
                g["sw"] = op.h
                g["sr"] = []
            else:
                ent = g["subs"].get(sub)
                if ent is None:
                    ent = g["subs"][sub] = [None, []]
                if ent[0] is not None:
                    deps.add(ent[0])
                elif g["sw"] is not None:
                    deps.add(g["sw"])
                deps.update(ent[1])
                deps.update(g["sr"])
                ent[0] = op.h
                ent[1] = []
        deps.discard(op.h)
        op.deps = deps
        self.q[eng].append(op)
        return op

    def finalize(self):
        for e in self.ENGS:
            for op in self.q[e]:
                for d in op.deps:
                    if d[0] == "c":
                        if e == "pe" and d[1] == "pe":
                            continue
                        self.q[d[1]][d[2]].sig = True
        self.rank = {}
        for e in self.ENGS:
            r = 0
            rk = []
            for op in self.q[e]:
                if op.sig:
                    r += 1
                rk.append(r)
            self.rank[e] = rk

    def emit_engine(self, e, eo, sem_eng, sem_dma, final_waits=()):
        waited = {}
        for op in self.q[e]:
            need = {}
            for d in op.deps:
                if d[0] == "c":
                    if e == "pe" and d[1] == "pe":
                        continue
                    key = ("c", d[1])
                    val = self.rank[d[1]][d[2]]
                else:
                    key = ("d", d[1])
                    val = 16 * d[2]
                if need.get(key, 0) < val:
                    need[key] = val
            for key, val in need.items():
                if waited.get(key, 0) < val:
                    sem = sem_eng[key[1]] if key[0] == "c" else sem_dma[key[1]]
                    eo.wait_ge(sem, val)
                    waited[key] = val
            ins = op.fn(eo)
            if op.dma is not None:
                ins.then_inc(sem_dma[op.dma], 16)
            elif op.sig:
                ins.then_inc(sem_eng[e], 1)
        for key in final_waits:
            eo.wait_ge(sem_dma[key], 16 * self.dma_n[key])


def build_nc():
    nc = bass.Bass("TRN2", target_bir_lowering=False)
    dr = {}

    def din(name, shape):
        dr[name] = nc.dram_tensor(name, list(shape), F32, kind="ExternalInput").ap()
        return dr[name]

    xT_d = din("xT", [NSEQ, 4, 128, 2, 2048])
    xtok_d = din("xtok", [NSEQ, S, D])
    win_d = din("w_in_r", [20, 128, 1024])
    wout_d = din("w_out_r", [128, 8, 1024])
    wffi_d = din("w_ffi_r", [NJ, 128, 2048])
    wffo_d = din("w_ffo_r", [128, NJ, 1024])
    chan_d = din("chanvec", [128, 36])
    lwa_d = din("lru_w_a", [8, 64, 64])
    lwx_d = din("lru_w_x", [8, 64, 64])
    dal_d = din("da_lambda", [1, 256])
    dag_d = din("da_norm_g", [1, 128])
    ln_d = [din(n, [1, 1024]) for n in ("ln1_g", "ln1_b", "ln2_g", "ln2_b")]
    out_d = nc.dram_tensor("out", [NSEQ, S, D], F32, kind="ExternalOutput").ap()

    P = Prog()

    R1N = 72 * 256
    R2N = 54 * 256
    R3N = 30 * 256
    with (
        nc.sbuf_tensor("R1", [128, R1N], F32) as R1,
        nc.sbuf_tensor("R2", [128, R2N], F32) as R2,
        nc.sbuf_tensor("R3", [128, R3N], F32) as R3,
        nc.sbuf_tensor("mg", [128, 8 * S], BF16) as mg_t,
        nc.sbuf_tensor("lnp", [128, 4 * 1024], F32) as lnp_t,
        nc.sbuf_tensor("ident", [128, 128], BF16) as ident_t,
        nc.sbuf_tensor("ones", [128, 128], BF16) as ones_t,
        nc.sbuf_tensor("tri2", [128, 256], BF16) as tri_t,
        nc.sbuf_tensor("chan", [128, 84], F32) as chan_t,
        nc.sbuf_tensor("wbda", [128, 4 * 128], BF16) as wbda_t,
        nc.sbuf_tensor("wbdx", [128, 4 * 128], BF16) as wbdx_t,
        nc.sbuf_tensor("gtile", [128, 128], F32) as gt_t,
        nc.psum_tensor("ps", [128, 8 * 512], F32) as ps_t,
    ):
        def view(arena, off_b, shape, dt):
            esz = 2 if dt == BF16 else 4
            n = 1
            for s_ in shape[1:]:
                n *= s_
            nb = n * esz
            assert off_b % 4 == 0 and nb % 4 == 0
            ap = arena[:, off_b // 4:(off_b + nb) // 4]
            if dt == BF16:
                ap = ap.bitcast(BF16)
            if len(shape) == 3:
                ap = ap.rearrange("p (a b) -> p a b", a=shape[1])
            elif len(shape) == 4:
                ap = ap.rearrange("p (a b c) -> p a b c", a=shape[1], b=shape[2])
            return ap

        KB = 1024
        xT = view(R1, 0, [128, 4, 8, 512], BF16)
        win = view(R1, 32 * KB, [128, 20, 8, 128], BF16)
        wout = view(R1, 0, [128, 8, 1024], BF16)
        wffo = view(R1, 16 * KB, [128, NJ, 1024], BF16)
        wsb = view(R1, 60 * KB, [128, 3, 2048], BF16)
        XP = 2052
        o = 0
        xpad = view(R2, o, [128, XP], F32); o += XP * 4
        xc = view(R2, o, [128, S], F32); o += S * 4
        ra = view(R2, o, [128, S], F32); o += S * 4
        ib = view(R2, o, [128, S], F32); o += S * 4
        mb = view(R2, o, [128, S], F32); o += S * 4
        gg = view(R2, o, [128, S], F32); o += S * 4
        xcb = view(R2, o, [128, S], BF16); o += S * 2
        assert o <= R2N * 4
        qkT = view(R2, 0, [128, 8, S], BF16)
        vaug = view(R2, 32 * KB, [128, 16, 4, 130], BF16)
        assert 32 * KB + 16 * 4 * 130 * 2 <= R2N * 4
        x1 = view(R2, 0, [128, 2, 4, 1024], F32)
        hT = view(R2, 32 * KB, [128, NJ, 512], BF16)
        Eb = view(R3, 0, [128, 3, 2, 512], BF16)
        of_ = view(R3, 6 * KB, [128, 4, 128], F32)
        onb = view(R3, 8 * KB, [128, 4, 128], BF16)
        junk = view(R3, 9 * KB, [128, 128], F32)
        ybuf = view(R3, 0, [128, 2, 1024], F32)
        osb = view(R3, 8 * KB, [128, 2, 1024], F32)
        x1b2 = view(R3, 28 * KB, [128, 1024], BF16)
        x1b = view(R3, 16 * KB, [128, 1024], BF16)
        sg = view(R3, 18 * KB, [128, 2, 512], BF16)
        xt = view(R3, 20 * KB, [128, 2, 1024], F32)
        mg = mg_t[:, :].rearrange("p (k t) -> p k t", k=8)
        lnp = lnp_t[:, :].rearrange("p (a d) -> p a d", a=4)
        ident = ident_t[:, :]
        ones = ones_t[:, :]
        tri2 = tri_t[:, :].rearrange("p (c j) -> p c j", c=2)
        chan = chan_t[:, 0:36].rearrange("p (c v) -> p c v", c=4)
        cneg = chan_t[:, 36:40]
        wbda = wbda_t[:, :].rearrange("p (c e) -> p c e", c=4)
        wbdx = wbdx_t[:, :].rearrange("p (c e) -> p c e", c=4)
        gtile = gt_t[:, :]
        lamt = view(R3, 10 * KB, [128, 256], F32)
        sm = chan_t[:, 40:84]
        neglam = sm[:, 0:1]
        s1 = sm[:, 1:2]
        s2 = sm[:, 2:3]
        zt = sm[:, 8:16]
        ss = sm[:, 16:20]
        rstd4 = sm[:, 20:24]
        bst = sm[:, 24:36]
        mv = sm[:, 36:38]
        lrstd = sm[:, 38:39]
        nmr = sm[:, 39:40]
        sp4 = sm[:, 40:44]
        cneg2 = sm[:, 3:7]
        ps = ps_t[:, :].rearrange("p (b n) -> p b n", b=8)
        psb4 = ps_t[:, 4 * 512:5 * 512].bitcast(BF16)
        psb7 = ps_t[:, 7 * 512:8 * 512].bitcast(BF16)

        def MG(G, k, t):
            return (("mg", G), k, t)

        def PS(b):
            return ("ps", b) if b < 5 else ("pb%d" % b, "*")

        def mm(out, lhsT, rhs, start, stop, reads, writes, skip=False):
            P.add("pe", lambda e: e.matmul(out, lhsT=lhsT, rhs=rhs, start=start, stop=stop,
                                           skip_group_check=skip), reads, writes)

        def tr(out, in_, reads, writes):
            P.add("pe", lambda e: e.transpose(out, in_, ident), reads, writes)

        def act(out, in_, func, reads, writes, bias=None, scale=None, accum_out=None):
            kw = {}
            if bias is not None:
                kw["bias"] = bias
            if scale is not None:
                kw["scale"] = scale
            if accum_out is not None:
                kw["accum_out"] = accum_out
            P.add("act", lambda e: e.activation(out=out, in_=in_, func=func, **kw), reads, writes)

        def ts(eng, out, in0, s1_, s2_, op0, op1, reads, writes):
            if op1 is None:
                P.add(eng, lambda e: e.tensor_scalar(out=out, in0=in0, scalar1=s1_, scalar2=None, op0=op0),
                      reads, writes)
            else:
                P.add(eng, lambda e: e.tensor_scalar(out=out, in0=in0, scalar1=s1_, scalar2=s2_, op0=op0, op1=op1),
                      reads, writes)

        def stt(out, in0, scalar, in1, op0, op1, reads, writes):
            P.add("dve", lambda e: e.scalar_tensor_tensor(out=out, in0=in0, scalar=scalar, in1=in1, op0=op0, op1=op1),
                  reads, writes)

        def tt(eng, out, in0, in1, op, reads, writes):
            P.add(eng, lambda e: e.tensor_tensor(out=out, in0=in0, in1=in1, op=op), reads, writes)

        def cp(eng, out, in_, reads, writes):
            if eng == "act":
                P.add("act", lambda e: e.copy(out=out, in_=in_), reads, writes)
            else:
                P.add(eng, lambda e: e.tensor_copy(out=out, in_=in_), reads, writes)

        def dma(eng, out, in_, key, reads, writes):
            P.add(eng, lambda e: e.dma_start(out=out, in_=in_), reads, writes, dma=key)

        def memset(eng, ap, val, reads, writes):
            P.add(eng, lambda e: e.memset(ap, val), reads, writes)

        I32 = mybir.dt.int32

        def rsqrt_dve(y, v, tmp, tok_y, tok_v, tok_t):
            vi = v.bitcast(I32)
            yi = y.bitcast(I32)
            P.add("dve", lambda e: e.tensor_single_scalar(out=yi, in_=vi, scalar=1, op=ALU.logical_shift_right),
                  [tok_v], [tok_y])
            P.add("dve", lambda e: e.tensor_scalar(out=yi, in0=yi, scalar1=-1.0, scalar2=float(0x5f3759df),
                                                   op0=ALU.mult, op1=ALU.add), [tok_y], [tok_y])
            for _ in range(3):
                tt("dve", tmp, y, y, ALU.mult, [tok_y], [tok_t])
                tt("dve", tmp, tmp, v, ALU.mult, [tok_t, tok_v], [tok_t])
                ts("dve", tmp, tmp, -0.5, 1.5, ALU.mult, ALU.add, [tok_t], [tok_t])
                tt("dve", y, y, tmp, ALU.mult, [tok_y, tok_t], [tok_y])

        memset("dve", ones, 1.0, [], [("c", "ones")])
        P.add("pool", lambda e: e.affine_select(out=ident, in_=ones, pattern=[[1, 128]], compare_op=ALU.is_equal,
                                                fill=0.0, base=0, channel_multiplier=-1),
              [("c", "ones")], [("c", "ident")])
        memset("pool", tri2[:, 0, :], 0.0, [], [("c", "tri0")])
        P.add("pool", lambda e: e.affine_select(out=tri2[:, 1, :], in_=tri2[:, 0, :], pattern=[[1, 128]],
                                                compare_op=ALU.is_ge, fill=-30000.0, base=0,
                                                channel_multiplier=-1),
              [("c", "tri0")], [("c", "tri")])
        dma("sp", chan_t[:, 0:36], chan_d[:, :], ("su", 0), [], [("c", "chan")])
        dma("sp", lamt, dal_d[0:1, :].partition_broadcast(128), ("su", 1), [], [("R3", "lamt")])
        dma("sp", gtile, dag_d[0:1, :].partition_broadcast(128), ("su", 2), [], [("c", "gtile")])
        for i in range(4):
            dma("sp", lnp[:, i, :], ln_d[i][0:1, :].partition_broadcast(128), ("su", 3 + i), [], [("c", "lnp", i)])
        act(sp4, chan[:, :, 7], AF.Exp, [("c", "chan")], [("c", "sp4")], scale=-1.0)
        act(sp4, sp4, AF.Ln, [("c", "sp4")], [("c", "sp4")], bias=1.0)
        ts("dve", cneg, sp4, -4.0, None, ALU.mult, None, [("c", "sp4")], [("c", "cneg")])
        ts("dve", cneg2, sp4, -8.0, None, ALU.mult, None, [("c", "sp4")], [("c", "cneg")])
        ts("dve", chan[:, :, 5:7], chan[:, :, 5:7], 0.5, None, ALU.mult, None, [("c", "chan")], [("c", "chan")])
        ts("dve", gtile, gtile, 1.0 - LAMBDA_INIT, None, ALU.mult, None, [("c", "gtile")], [("c", "gtile")])
        tt("dve", junk[:, 0:64], lamt[:, 0:64], lamt[:, 64:128], ALU.mult, [("R3", "lamt")], [("R3", "junk")])
        P.add("dve", lambda e: e.reduce_sum(out=s1, in_=junk[:, 0:64], axis=AX.X), [("R3", "junk")], [("c", "s1")])
        tt("dve", junk[:, 64:128], lamt[:, 128:192], lamt[:, 192:256], ALU.mult, [("R3", "lamt")], [("R3", "junk2")])
        P.add("dve", lambda e: e.reduce_sum(out=s2, in_=junk[:, 64:128], axis=AX.X), [("R3", "junk2")], [("c", "s2")])
        act(s1, s1, AF.Exp, [("c", "s1")], [("c", "s1")])
        act(s2, s2, AF.Exp, [("c", "s2")], [("c", "s2")])
        tt("dve", neglam, s2, s1, ALU.subtract, [("c", "s1"), ("c", "s2")], [("c", "neglam")])
        ts("dve", neglam, neglam, -LAMBDA_INIT, None, ALU.add, None, [("c", "neglam")], [("c", "neglam")])

        wst = view(R3, 20 * KB, [128, 2, 4, 128], F32)

        def load_wbd():
            memset("dve", wst, 0.0, [], [("R3", "wst", gi, n) for gi in range(2) for n in range(8)])
            for gi, src in enumerate((lwa_d, lwx_d)):
                for n in range(8):
                    cc, hf = n // 2, n % 2
                    dma("sp", wst[hf * 64:(hf + 1) * 64, gi, cc, hf * 64:(hf + 1) * 64], src[n, :, :],
                        ("su", 7 + 8 * gi + n), [], [("R3", "wst", gi, n)])

        def cast_wbd():
            cp("dve", wbda, wst[:, 0, :, :], [("R3", "wst", 0, n) for n in range(8)], [("c", "wbda")])
            cp("dve", wbdx, wst[:, 1, :, :], [("R3", "wst", 1, n) for n in range(8)], [("c", "wbdx")])

        rot = {"a2": 0, "a1": 0, "ev": 0}

        def load_win(blk, star=False):
            w = [("R1", "*")] if star else [("R1", "win", blk)]
            dma("pool", win[:, blk, :, :].rearrange("p k c -> p (k c)"), win_d[blk], ("win", blk), [], w)

        def load_xT(s, tg, star=False, alias=()):
            w = [("R1", "*")] if star else [("R1", "xT", tg)] + list(alias)
            dma("pool", xT[:, tg, :, :].rearrange("p (a k) t -> p a (k t)", a=2), xT_d[s, tg, :, :, :],
                ("xT", tg), [], w)

        def load_seq_inputs(s):
            if s == 0:
                load_xT(s, 0)
                load_win(12)
                load_win(16)
                load_xT(s, 1)
            else:
                load_xT(s, 0, alias=[("R1", "wout", k) for k in range(0, 4)])
                load_xT(s, 1, alias=[("R1", "wout", k) for k in range(4, 8)])
                load_win(12, star=True)
                load_win(16)
            for tg in range(2, 4):
                load_xT(s, tg)
            for cc in range(1, 4):
                load_win(12 + cc)
                load_win(16 + cc)
            for blk in range(12):
                load_win(blk)

        def phase_lru(s):
            gg2 = view(R3, 0, [128, 2, S], F32)
            sqb = view(R3, 16 * KB, [128, S], BF16)
            r3_first = [True]
            TS = [slice(tg * 512, (tg + 1) * 512) for tg in range(4)]
            memset("dve", xpad[:, 0:3], 0.0, [], [("R2", "*")])

            def proj(blk, tg, b):
                for k in range(8):
                    mm(ps[:, b, :], win[:, blk, k, :], xT[:, tg, k, :], k == 0, k == 7,
                       [("R1", "win", blk), ("R1", "xT", tg)], [PS(b)])

            PB = [0, 1, 4, 5, 6]

            def head(cc):
                par = cc % 2
                for tg in range(4):
                    b = PB[rot["a2"] % 5]
                    rot["a2"] += 1
                    proj(12 + cc, tg, b)
                    cp("act", xpad[:, 3 + tg * 512:3 + (tg + 1) * 512], ps[:, b, :], [PS(b)], [("R2", "xpad", tg)])
                for tg in range(4):
                    b = PB[rot["a2"] % 5]
                    rot["a2"] += 1
                    proj(16 + cc, tg, b)
                    w = [("R3", "*")] if r3_first[0] else [("R3", "gg", par, tg)]
                    r3_first[0] = False
                    act(gg2[:, par, TS[tg]], ps[:, b, :], AF.Gelu_apprx_tanh, [PS(b)], w)

            def conv_gates(cc, tg):
                tsl = TS[tg]
                xr = [("R2", "xpad", tg)] + ([("R2", "xpad", tg - 1)] if tg > 0 else [])
                ts("dve", xc[:, tsl], xpad[:, 3 + tg * 512:3 + (tg + 1) * 512], chan[:, cc, 3:4], chan[:, cc, 4:5],
                   ALU.mult, ALU.add, xr + [("c", "chan")], [("R2", "xc", tg)])
                for tap in (2, 1, 0):
                    stt(xc[:, tsl], xpad[:, tap + tg * 512:tap + (tg + 1) * 512], chan[:, cc, tap:tap + 1],
                        xc[:, tsl], ALU.mult, ALU.add, xr + [("R2", "xc", tg)], [("R2", "xc", tg)])
                cp("act", xcb[:, tsl], xc[:, tsl], [("R2", "xc", tg)], [("R2", "xcb", tg)])
                mm(ps[:, 2, :], wbda[:, cc, :], xcb[:, tsl], True, True, [("c", "wbda"), ("R2", "xcb", tg)], [PS(2)])
                act(ra[:, tsl], ps[:, 2, :], AF.Tanh, [PS(2), ("c", "chan")], [("R2", "ra", tg)],
                    bias=chan[:, cc, 5:6], scale=0.5)
                mm(ps[:, 3, :], wbdx[:, cc, :], xcb[:, tsl], True, True, [("c", "wbdx"), ("R2", "xcb", tg)], [PS(3)])
                act(ib[:, tsl], ps[:, 3, :], AF.Tanh, [PS(3), ("c", "chan")], [("R2", "ib", tg)],
                    bias=chan[:, cc, 6:7], scale=0.5)
                act(mb[:, tsl], ra[:, tsl], AF.Exp, [("R2", "ra", tg), ("c", "cneg")], [("R2", "mb", tg)],
                    bias=cneg2[:, cc:cc + 1], scale=cneg2[:, cc:cc + 1])
                act(ra[:, tsl], ra[:, tsl], AF.Exp, [("R2", "ra", tg), ("c", "cneg")], [("R2", "ra", tg)],
                    bias=cneg[:, cc:cc + 1], scale=cneg[:, cc:cc + 1])

            def tail_a(cc):
                for tg in range(4):
                    tsl = TS[tg]
                    act(mb[:, tsl], mb[:, tsl], AF.Sqrt, [("R2", "mb", tg)], [("R2", "mb", tg)], bias=1.0, scale=-1.0)
                memset("dve", mb[:, 0:1], 1.0, [("R2", "mb", 0)], [("R2", "mb", 0)])

            carry = sm[:, 40:41]

            def tail_tg(cc, tg):
                par = cc % 2
                tsl = TS[tg]
                stt(ib[:, tsl], ib[:, tsl], 1.0, xc[:, tsl], ALU.add, ALU.mult,
                    [("R2", "ib", tg), ("R2", "xc", tg)], [("R2", "ib", tg)])
                stt(ib[:, tsl], ib[:, tsl], 0.5, mb[:, tsl], ALU.mult, ALU.mult,
                    [("R2", "ib", tg), ("R2", "mb", tg)], [("R2", "ib", tg)])
                if tg == 0:
                    P.add("dve", lambda e: e.tensor_tensor_scan(
                        out=xc[:, tsl], data0=ra[:, tsl], data1=ib[:, tsl], initial=0.0,
                        op0=ALU.mult, op1=ALU.add),
                        [("R2", "ra", tg), ("R2", "ib", tg), ("R2", "xc", tg)], [("R2", "xc", tg)])
                else:
                    P.add("dve", lambda e: e.tensor_tensor_scan(
                        out=xc[:, tsl], data0=ra[:, tsl], data1=ib[:, tsl], initial=carry,
                        op0=ALU.mult, op1=ALU.add),
                        [("R2", "ra", tg), ("R2", "ib", tg), ("R2", "xc", tg), ("c", "carry")], [("R2", "xc", tg)])
                if tg < 3:
                    cp("dve", carry, xc[:, (tg + 1) * 512 - 1:(tg + 1) * 512], [("R2", "xc", tg)], [("c", "carry")])
                mgw = [MG(tg, 4 + cc, t) for t in range(4)]
                tt("dve", mg[:, 4 + cc, tsl], xc[:, tsl], gg2[:, par, tsl], ALU.mult,
                   [("R2", "xc", tg), ("R3", "gg", par, tg)], mgw)

            head(0)
            if s == 0:
                cast_wbd()
            for tg in range(4):
                conv_gates(0, tg)
            for cc in range(4):
                tail_a(cc)
                if cc + 1 < 4:
                    head(cc + 1)
                for tg in range(4):
                    tail_tg(cc, tg)
                    if cc + 1 < 4:
                        conv_gates(cc + 1, tg)
            rs = gg2[:, 0, :]
            units = []

            def u_sq(tg, cc):
                def f():
                    tsl = TS[tg]
                    mgw = [MG(tg, 4 + cc, t) for t in range(4)]
                    act(sqb[:, TS[cc]], mg[:, 4 + cc, tsl], AF.Square, mgw, [("R3", "sq", cc)])
                    mm(ps[:, 4 + tg, :], ones, sqb[:, TS[cc]], cc == 0, cc == 3, [("c", "ones"), ("R3", "sq", cc)],
                       [PS(4 + tg)])
                return f

            def u_rs(tg):
                def f():
                    tsl = TS[tg]
                    act(rs[:, tsl], ps[:, 4 + tg, :], AF.Sqrt, [PS(4 + tg)], [("R3", "gg", 0, tg)],
                        bias=RMS_EPS, scale=1.0 / 512.0)
                    P.add("dve", lambda e: e.reciprocal(out=rs[:, tsl], in_=rs[:, tsl]),
                          [("R3", "gg", 0, tg)], [("R3", "gg", 0, tg)])
                return f

            def u_scale(tg, cc):
                def f():
                    tsl = TS[tg]
                    mgw = [MG(tg, 4 + cc, t) for t in range(4)]
                    stt(mg[:, 4 + cc, tsl], mg[:, 4 + cc, tsl], chan[:, cc, 8:9], rs[:, tsl], ALU.mult, ALU.mult,
                        mgw + [("R3", "gg", 0, tg), ("c", "chan")], mgw)
                return f

            for tg in range(4):
                for cc in range(4):
                    units.append(u_sq(tg, cc))
            for tg in range(4):
                units.append(u_rs(tg))
                for cc in range(4):
                    units.append(u_scale(tg, cc))
            return units

        def phase_qkv(s, units):
            memset("dve", vaug[:, :, :, 128:129], 1.0, [], [("R2", "*")])
            for ch in range(8):
                for tg in range(4):
                    tsl = slice(tg * 512, (tg + 1) * 512)
                    b = rot["a1"] % 4
                    rot["a1"] += 1
                    for k in range(8):
                        mm(ps[:, b, :], win[:, ch, k, :], xT[:, tg, k, :], k == 0, k == 7,
                           [("R1", "win", ch), ("R1", "xT", tg)], [PS(b)])
                    eng = "act" if rot["ev"] % 2 == 0 else "dve"
                    rot["ev"] += 1
                    cp(eng, qkT[:, ch, tsl], ps[:, b, :], [PS(b)], [("R2", "qk", ch, tg)])
                    if units:
                        units.pop(0)()
            for t16 in range(16):
                b = rot["a1"] % 4
                rot["a1"] += 1
                for k in range(8):
                    mm(ps[:, b, :].rearrange("p (h e) -> p h e", h=4),
                       xT[:, t16 // 4, k, (t16 % 4) * 128:(t16 % 4 + 1) * 128],
                       win[:, 8:12, k, :], k == 0, k == 7,
                       [("R1", "win", 8 + i) for i in range(4)] + [("R1", "xT", t16 // 4)], [PS(b)])
                eng = "act" if rot["ev"] % 2 == 0 else "dve"
                rot["ev"] += 1
                cp(eng, vaug[:, t16, :, 0:128], ps[:, b, :].rearrange("p (h e) -> p h e", h=4), [PS(b)],
                   [("R2", "v", t16)])
                if units:
                    units.pop(0)()
            while units:
                units.pop(0)()

        def load_cd_weights(s):
            for k in range(8):
                w = [("R1", "*")] if k == 0 else [("R1", "wout", k)]
                dma("pool", wout[:, k, :], wout_d[:, k, :], ("wout", k), [], w)
            for jj in range(NJ // 2):
                dma("pool", wffo[:, 2 * jj:2 * jj + 2, :], wffo_d[:, 2 * jj:2 * jj + 2, :], ("wffo", jj), [],
                    [("R1", "wffo", jj)])

        def acc_ap(a, lo, hi):
            b = 5 + a // 3
            off = (a % 3) * 129
            return ps[:, b, off + lo:off + hi]

        def acc_tok(a):
            return ("pb%d" % (5 + a // 3), a)

        def phase_attn(s):
            first_e = [True]
            steps = []
            for h in range(4):
                for qg in range(4):
                    nfull = 4 * qg
                    lst = [(kb, -1) for kb in range(nfull)] + [(nfull + i, i) for i in range(4)]
                    for si, (kb, i) in enumerate(lst):
                        steps.append((h, qg, kb, i, si == 0, si == len(lst) - 1))
            ecnt = [0]

            def emit_qk(n):
                h, qg, kb, i, _, _ = steps[n]
                sp_ = n % 2
                col0 = 0 if i < 0 else i * 128
                qtok = [("R2", "qk", 4 + h, kb // 4), ("R2", "qk", h, qg)]
                for c in range(2):
                    kT_c = qkT[c * 64:(c + 1) * 64, 4 + h, kb * 128:(kb + 1) * 128]
                    bank = 2 * sp_ + c
                    if i < 0:
                        mm(ps[:, bank, 0:512], kT_c, qkT[c * 64:(c + 1) * 64, h, qg * 512:(qg + 1) * 512],
                           True, True, qtok, [("ps", bank)])
                    else:
                        q0 = qg * 512 + col0
                        mm(ps[:, bank, col0:col0 + 128], kT_c, qkT[c * 64:(c + 1) * 64, h, q0:q0 + 128],
                           True, False, qtok, [("ps", bank)])
                        mm(ps[:, bank, col0:col0 + 128], ident, tri2[:, 1, :], False, True,
                           [("c", "ident"), ("c", "tri")], [("ps", bank)])
                        if col0 + 128 < 512:
                            mm(ps[:, bank, col0 + 128:512], kT_c,
                               qkT[c * 64:(c + 1) * 64, h, q0 + 128:(qg + 1) * 512],
                               True, True, qtok, [("ps", bank)])

            def emit_exp(n):
                h, qg, kb, i, _, _ = steps[n]
                sp_ = n % 2
                eb = n % 3
                col0 = 0 if i < 0 else i * 128
                w = [("R3", "*")] if first_e[0] else [("R3", "E", eb)]
                first_e[0] = False
                act(Eb[:, eb, :, col0:512], ps[:, 2 * sp_:2 * sp_ + 2, col0:512], AF.Exp,
                    [("ps", 2 * sp_), ("ps", 2 * sp_ + 1)], w, scale=0.125)

            def emit_pv(n):
                h, qg, kb, i, is_first, is_last = steps[n]
                eb = n % 3
                qb0 = 0 if i < 0 else i
                for qb in range(qb0, 4):
                    for c in range(2):
                        a = qb * 2 + c
                        st = is_first and (a % 3 == 0)
                        sp = (kb == 4 * qg + qb)
                        mm(acc_ap(a, 0, 129), Eb[:, eb, c, qb * 128:(qb + 1) * 128], vaug[:, kb, h, 0:129],
                           st, sp, [("R3", "E", eb), ("R2", "v", kb)], [acc_tok(a)], skip=True)
                if is_last:
                    finalize(h, qg, n)
                while pend and pend[0][0] <= n:
                    fin_b(*pend.pop(0)[1])

            accsb = view(R3, 10 * KB, [128, 3, 387], F32)
            onb2 = view(R3, 16 * KB, [128, 2, 4, 128], BF16)
            pend = []
            gcnt = [0]

            def finalize(h, qg, n):
                while len(pend) > 1:
                    fin_b(*pend.pop(0)[1])
                gp = gcnt[0] % 2
                gcnt[0] += 1
                for bi in range(3):
                    na = 3 if bi < 2 else 2
                    cp("dve", accsb[:, bi, 0:na * 129], ps[:, 5 + bi, 0:na * 129],
                       [acc_tok(3 * bi + j) for j in range(na)], [("R3", "accsb", bi)])
                av = accsb.rearrange("p b (a c) -> p b a c", c=129)
                zt4 = zt.rearrange("p (a o) -> p a o", o=1)
                P.add("dve", lambda e: e.reciprocal(out=zt[:, 0:6].rearrange("p (b a o) -> p b a o", b=2, o=1),
                                                    in_=av[:, 0:2, :, 128:129]),
                      [("R3", "accsb", 0), ("R3", "accsb", 1)], [("c", "zt")])
                P.add("dve", lambda e: e.reciprocal(out=zt4[:, 6:8, :], in_=av[:, 2, 0:2, 128:129]),
                      [("R3", "accsb", 2)], [("c", "zt")])
                ztv = zt.rearrange("p (q c) -> p q c", c=2)
                ts("dve", ztv[:, :, 1:2], ztv[:, :, 1:2], neglam, None, ALU.mult, None,
                   [("c", "zt"), ("c", "neglam")], [("c", "zt")])

                def acs(a):
                    return accsb[:, a // 3, (a % 3) * 129:(a % 3) * 129 + 128]

                for qb in range(4):
                    rd = [("R3", "accsb", (2 * qb) // 3), ("R3", "accsb", (2 * qb + 1) // 3), ("c", "zt")]
                    ts("dve", of_[:, qb, :], acs(2 * qb), zt[:, 2 * qb:2 * qb + 1], None, ALU.mult, None,
                       rd, [("R3", "of", qb)])
                    stt(of_[:, qb, :], acs(2 * qb + 1), zt[:, 2 * qb + 1:2 * qb + 2], of_[:, qb, :],
                        ALU.mult, ALU.add, rd + [("R3", "of", qb)], [("R3", "of", qb)])
                    P.add("dve", lambda e, qb=qb: e.scalar_tensor_tensor(
                        out=junk, in0=of_[:, qb, :], scalar=1.0, in1=of_[:, qb, :], op0=ALU.mult, op1=ALU.mult,
                        accum_out=ss[:, qb:qb + 1]), [("R3", "of", qb)], [("R3", "junk"), ("c", "ss")])
                ts("dve", ss, ss, 1.0 / 128.0, RMS_EPS, ALU.mult, ALU.add, [("c", "ss")], [("c", "ss")])
                rsqrt_dve(rstd4, ss, zt[:, 0:4], ("c", "rstd4"), ("c", "ss"), ("c", "zt"))
                for qb in range(4):
                    stt(onb2[:, gp, qb, :], of_[:, qb, :], rstd4[:, qb:qb + 1], gtile, ALU.mult, ALU.mult,
                        [("R3", "of", qb), ("c", "rstd4"), ("c", "gtile")], [("R3", "onb", gp, qb)])
                pend.append((n + 14, (h, qg, gp)))

            def fin_b(h, qg, gp):
                for qb in range(4):
                    tr(psb4[:, qb * 128:(qb + 1) * 128], onb2[:, gp, qb, :],
                       [("R3", "onb", gp, qb), ("c", "ident")], [PS(4)])
                cp("dve", mg[:, h, qg * 512:(qg + 1) * 512], psb4[:, 0:512], [PS(4)],
                   [MG(qg, h, t) for t in range(4)])

            N = len(steps)
            emit_qk(0)
            for n in range(N):
                emit_exp(n)
                if n + 1 < N:
                    emit_qk(n + 1)
                emit_pv(n)
            while pend:
                fin_b(*pend.pop(0)[1])

        def lnsc(p):
            if p == 0:
                return dict(bst=sm[:, 24:36], mv=sm[:, 36:38], lr=sm[:, 38:39], nm=sm[:, 39:40], tmp=sm[:, 1:2],
                            t_bst=[("c", "bst0")], t_mv=("c", "mv0"), t_lr=("c", "lr0"), t_nm=("c", "nm0"),
                            t_tmp=("c", "tmp0"))
            return dict(bst=sm[:, 8:20], mv=sm[:, 20:22], lr=sm[:, 22:23], nm=sm[:, 23:24], tmp=sm[:, 7:8],
                        t_bst=[("c", "zt"), ("c", "ss")], t_mv=("c", "rstd4"), t_lr=("c", "rstd4"),
                        t_nm=("c", "rstd4"), t_tmp=("c", "tmp1"))

        def rsqrt_col(y, v, tmp, tok_y, tok_v, tok_t):
            vi = v.bitcast(I32)
            yi = y.bitcast(I32)
            P.add("dve", lambda e: e.tensor_single_scalar(out=yi, in_=vi, scalar=1, op=ALU.logical_shift_right),
                  [tok_v], [tok_y])
            P.add("dve", lambda e: e.tensor_scalar(out=yi, in0=yi, scalar1=-1.0, scalar2=float(0x5f3759df),
                                                   op0=ALU.mult, op1=ALU.add), [tok_y], [tok_y])
            for _ in range(3):
                stt(tmp, y, v, y, ALU.mult, ALU.mult, [tok_y, tok_v], [tok_t])
                ts("dve", tmp, tmp, -0.5, 1.5, ALU.mult, ALU.add, [tok_t], [tok_t])
                tt("dve", y, y, tmp, ALU.mult, [tok_y, tok_t], [tok_y])

        def ln_stats(yb, y_tok, sc):
            for c in range(2):
                P.add("dve", lambda e, c=c: e.bn_stats(out=sc["bst"][:, 6 * c:6 * c + 6],
                                                       in_=yb[:, c * 512:(c + 1) * 512]),
                      [y_tok], sc["t_bst"])
            P.add("dve", lambda e: e.bn_aggr(out=sc["mv"], in_=sc["bst"]), sc["t_bst"], [sc["t_mv"]])
            ts("dve", sc["mv"][:, 1:2], sc["mv"][:, 1:2], LN_EPS, None, ALU.add, None, [sc["t_mv"]], [sc["t_mv"]])

        def ln_rstd(sc):
            rsqrt_col(sc["lr"], sc["mv"][:, 1:2], sc["tmp"], sc["t_lr"], sc["t_mv"], sc["t_tmp"])
            stt(sc["nm"], sc["mv"][:, 0:1], -1.0, sc["lr"], ALU.mult, ALU.mult, [sc["t_mv"], sc["t_lr"]],
                [sc["t_nm"]])

        def ln_norm(yb, y_tok, sc):
            act(yb, yb, AF.Identity, [y_tok, sc["t_lr"], sc["t_nm"]], [y_tok], bias=sc["nm"], scale=sc["lr"])

        def ln_affine(yb, y_tok, gi, bi_, dst, dst_tok, eng="dve"):
            tt(eng, dst, yb, lnp[:, gi, :], ALU.mult, [y_tok, ("c", "lnp", gi)], [dst_tok])
            tt(eng, dst, dst, lnp[:, bi_, :], ALU.add, [dst_tok, ("c", "lnp", bi_)], [dst_tok])

        def layer_norm(yb, gi, bi_, dst, y_tok, dst_tok, sc):
            ln_stats(yb, y_tok, sc)
            ln_rstd(sc)
            ln_norm(yb, y_tok, sc)
            ln_affine(yb, y_tok, gi, bi_, dst, dst_tok)

        cnt = {"xt": 0, "y": 0, "o": 0, "ws": 0, "gu": 0, "sg": 0, "acc": 0}

        def phase_cd(s):
            st = {"r2_first": True, "r3_first": True}

            ctxs = {}

            def c_stage(G, t, d):
                t16 = 4 * G + t
                par = G % 2
                if d == 0:
                    xs = cnt["xt"] % 2
                    cnt["xt"] += 1
                    w = [("R3", "*")] if st["r3_first"] else [("R3", "xt", xs)]
                    st["r3_first"] = False
                    dma("sp", xt[:, xs, :], xtok_d[s, t16 * 128:(t16 + 1) * 128, :], ("xt", xs), [], w)
                    ab = 2 * (cnt["acc"] % 2)
                    cnt["acc"] += 1
                    for hf in range(2):
                        for k in range(8):
                            mm(ps[:, ab + hf, :], mg[:, k, t16 * 128:(t16 + 1) * 128],
                               wout[:, k, hf * 512:(hf + 1) * 512], k == 0, k == 7,
                               [MG(G, k, t), ("R1", "wout", k)], [PS(ab + hf)])
                    ctxs[(G, t)] = dict(xs=xs, ab=ab, ys=t % 2, sc=lnsc(t % 2))
                    return
                c = ctxs[(G, t)]
                ys = c["ys"]
                yb = ybuf[:, ys, :]
                ytok = ("R3", "y", ys)
                if d == 1:
                    ab = c["ab"]
                    stt(yb, xt[:, c["xs"], :], ALPHA, ps_t[:, ab * 512:(ab + 2) * 512], ALU.mult, ALU.add,
                        [("R3", "xt", c["xs"]), PS(ab), PS(ab + 1)], [ytok])
                    ln_stats(yb, ytok, c["sc"])
                elif d == 2:
                    ln_rstd(c["sc"])
                elif d == 3:
                    ln_norm(yb, ytok, c["sc"])
                elif d == 4:
                    dst_tok = ("R2", "*") if st["r2_first"] else ("R2", "x1", par, t)
                    st["r2_first"] = False
                    ln_affine(yb, ytok, 0, 1, x1[:, par, t, :], dst_tok, eng="pool" if G == 0 else "dve")
                elif d == 5:
                    xb_ = x1b if t % 2 == 0 else x1b2
                    cp("act", xb_, x1[:, par, t, :], [("R2", "x1", par, t)], [("R3", "x1b", t % 2)])
                elif d == 6:
                    xb_ = x1b if t % 2 == 0 else x1b2
                    for k in range(8):
                        tr(psb7[:, k * 128:(k + 1) * 128], xb_[:, k * 128:(k + 1) * 128],
                           [("R3", "x1b", t % 2), ("c", "ident")], [PS(7)])
                    cp("act" if G == 0 else "dve", mg[:, :, t16 * 128:(t16 + 1) * 128],
                       psb7.rearrange("p (k t) -> p k t", k=8), [PS(7)], [MG(G, k, t) for k in range(8)])

            for v in range(0, 6 + 2 * 3 + 1):
                for t in range(4):
                    d = v - 2 * t
                    if 0 <= d <= 6:
                        c_stage(0, t, d)
            ln2_def = {}
            for G in range(4):
                par = G % 2
                x1T_r = [(("mg", G), "*")]
                for j in range(NJ):
                    wsl = cnt["ws"] % 3
                    cnt["ws"] += 1
                    dma("pool", wsb[:, wsl, :], wffi_d[j, :, :], ("ws", wsl), [], [("R1", "ws", wsl)])
                    wv = wsb[:, wsl, :].rearrange("p (k c) -> p k c", k=8)
                    gbk = []
                    for u in range(2):
                        gb = 4 + (cnt["gu"] % 3)
                        cnt["gu"] += 1
                        gbk.append(gb)
                        for k in range(8):
                            mm(ps[:, gb, :], wv[:, k, u * 128:(u + 1) * 128], mg[:, k, G * 512:(G + 1) * 512],
                               k == 0, k == 7, [("R1", "ws", wsl)] + x1T_r, [PS(gb)])
                    sgs = cnt["sg"] % 2
                    cnt["sg"] += 1
                    act(sg[:, sgs, :], ps[:, gbk[0], :], AF.Silu, [PS(gbk[0])], [("R3", "sg", sgs)])
                    tt("dve", hT[:, j, :], sg[:, sgs, :], ps[:, gbk[1], :], ALU.mult,
                       [("R3", "sg", sgs), PS(gbk[1])], [("R2", "hT", j)])
                    for f_ in ln2_def.pop(j, []):
                        f_()
                    if G + 1 < 4:
                        for t in range(4):
                            d = j - 5 * t
                            if 0 <= d <= 6:
                                c_stage(G + 1, t, d)
                for t in range(4):
                    t16 = 4 * G + t
                    ab = 2 * (cnt["acc"] % 2)
                    cnt["acc"] += 1
                    for hf in range(2):
                        for j in range(NJ):
                            mm(ps[:, ab + hf, :], hT[:, j, t * 128:(t + 1) * 128],
                               wffo[:, j, hf * 512:(hf + 1) * 512], j == 0, j == NJ - 1,
                               [("R2", "hT", j), ("R1", "wffo", j // 2)], [PS(ab + hf)])
                    ys = t % 2
                    os_ = cnt["o"] % 2
                    cnt["o"] += 1
                    sc_ = lnsc(t % 2)

                    def l0(ys=ys, ab=ab, par=par, t=t, sc_=sc_):
                        stt(ybuf[:, ys, :], x1[:, par, t, :], ALPHA, ps_t[:, ab * 512:(ab + 2) * 512],
                            ALU.mult, ALU.add, [("R2", "x1", par, t), PS(ab), PS(ab + 1)], [("R3", "y", ys)])
                        ln_stats(ybuf[:, ys, :], ("R3", "y", ys), sc_)

                    def l1(sc_=sc_):
                        ln_rstd(sc_)

                    def l2(ys=ys, sc_=sc_):
                        ln_norm(ybuf[:, ys, :], ("R3", "y", ys), sc_)

                    def l3(ys=ys, os_=os_, t16=t16):
                        ln_affine(ybuf[:, ys, :], ("R3", "y", ys), 2, 3, osb[:, os_, :], ("R3", "o", os_))
                        dma("sp", out_d[s, t16 * 128:(t16 + 1) * 128, :], osb[:, os_, :], ("st", os_),
                            [("R3", "o", os_)], [])

                    if t == 3 and G < 3:
                        ln2_def[0] = [l0]
                        ln2_def[1] = [l1]
                        ln2_def[2] = [l2]
                        ln2_def[3] = [l3]
                    else:
                        l0()
                        l1()
                        l2()
                        l3()

        for s in range(NSEQ):
            if s == 0:
                load_wbd()
            load_seq_inputs(s)
            units = phase_lru(s)
            phase_qkv(s, units)
            load_cd_weights(s)
            phase_attn(s)
            phase_cd(s)

        P.finalize()
        keys = sorted(P.dma_n.keys(), key=str)
        sem_eng = {}
        sem_dma = {}
        for e in ("pe", "act", "dve", "pool"):
            sem_eng[e] = nc.alloc_semaphore(name="se_" + e)
        for i, k in enumerate(keys):
            sem_dma[k] = nc.alloc_semaphore(name="sd_%d" % i)
        with nc.Block() as block:
            @block.tensor
            def _(e):
                P.emit_engine("pe", e, sem_eng, sem_dma)

            @block.scalar
            def _(e):
                P.emit_engine("act", e, sem_eng, sem_dma)

            @block.vector
            def _(e):
                P.emit_engine("dve", e, sem_eng, sem_dma)

            @block.gpsimd
            def _(e):
                P.emit_engine("pool", e, sem_eng, sem_dma)

            @block.sync
            def _(e):
                P.emit_engine("sp", e, sem_eng, sem_dma, final_waits=[("st", 0), ("st", 1)])
    return nc


def _prep_shared(inp):
    f = lambda a: np.ascontiguousarray(np.asarray(a, dtype=np.float32))
    w_in = f(inp["w_in"])[0]
    w_out = f(inp["w_out"])[0]
    w_ffi = f(inp["w_ffn_in"])[0]
    w_ffo = f(inp["w_ffn_out"])[0]
    sh = {}
    sh["w_in_r"] = f(w_in.reshape(8, 128, 20, 128).transpose(2, 1, 0, 3).reshape(20, 128, 1024))
    sh["w_out_r"] = f(w_out.reshape(8, 128, 1024).transpose(1, 0, 2))
    g = w_ffi[:, :DFF].reshape(8, 128, NJ, 128)
    u = w_ffi[:, DFF:].reshape(8, 128, NJ, 128)
    gu = np.concatenate([g, u], axis=3)
    sh["w_ffi_r"] = f(gu.transpose(2, 1, 0, 3).reshape(NJ, 128, 2048))
    sh["w_ffo_r"] = f(w_ffo.reshape(NJ, 128, 1024).transpose(1, 0, 2))
    cv = np.zeros((128, 4, 9), np.float32)
    conv_w = f(inp["conv_w"])[0]
    for tap in range(4):
        cv[:, :, tap] = conv_w[tap].reshape(4, 128).T
    for i, n in enumerate(["conv_b", "lru_b_a", "lru_b_x", "lru_lambda", "rec_norm_g"]):
        cv[:, :, 4 + i] = f(inp[n])[0].reshape(4, 128).T
    sh["chanvec"] = f(cv.reshape(128, 36))
    sh["lru_w_a"] = f(inp["lru_w_a"])[0]
    sh["lru_w_x"] = f(inp["lru_w_x"])[0]
    sh["da_lambda"] = f(inp["da_lambda"])[0].reshape(1, 256)
    sh["da_norm_g"] = f(inp["da_norm_g"])[0].reshape(1, 128)
    for n in ("ln1_g", "ln1_b", "ln2_g", "ln2_b"):
        sh[n] = f(inp[n])[0].reshape(1, 1024)
    return sh


def kernel(**inputs):
    x = np.asarray(inputs["x"], dtype=np.float32)
    sh = _prep_shared(inputs)
    in_maps = []
    for c in range(NCORES):
        xc = x[c * NSEQ:(c + 1) * NSEQ]
        xT = np.ascontiguousarray(xc.reshape(NSEQ, 4, 512, 8, 128).transpose(0, 1, 4, 3, 2)).reshape(
            NSEQ, 4, 128, 2, 2048)
        m = dict(sh)
        m["xT"] = xT
        m["xtok"] = np.ascontiguousarray(xc)
        in_maps.append(m)
    nc = build_nc()
    res = run_bass_kernel_spmd(nc, in_maps, core_ids=list(range(NCORES)))
    out = np.concatenate([np.asarray(r["out"], dtype=np.float32) for r in res.results], axis=0)
    return out


if __name__ == "__main__":
    import time
    t0 = time.time()
    nc = build_nc()
    print("build ok", time.time() - t0)
```

```python
import numpy as np
import concourse.bass as bass
import concourse.mybir as mybir
from concourse.bass_utils import run_bass_kernel_spmd

F32 = mybir.dt.float32
BF16 = mybir.dt.bfloat16
AF = mybir.ActivationFunctionType
ALU = mybir.AluOpType
AX = mybir.AxisListType

NCORES = 8
NSEQ = 2
S = 2048
D = 1024
DFF = 2816
NJ = DFF // 128
ALPHA = float(2.0 ** 0.25)
LAMBDA_INIT = 0.2
LN_EPS = 1e-5
RMS_EPS = 1e-5


class _Op:
    __slots__ = ("eng", "fn", "idx", "dma", "h", "deps", "sig")


class Prog:
    ENGS = ("pe", "act", "dve", "pool", "sp")

    def __init__(self):
        self.q = {e: [] for e in self.ENGS}
        self.tok = {}
        self.dma_n = {}

    def _grp(self, g):
        return self.tok.setdefault(g, {"sw": None, "sr": [], "subs": {}})

    def add(self, eng, fn, reads=(), writes=(), dma=None):
        op = _Op()
        op.eng = eng
        op.fn = fn
        op.idx = len(self.q[eng])
        op.dma = dma
        op.sig = False
        if dma is not None:
            n = self.dma_n.get(dma, 0) + 1
            self.dma_n[dma] = n
            op.h = ("d", dma, n)
        else:
            op.h = ("c", eng, op.idx)
        deps = set()
        for t in reads:
            g = self._grp(t[0])
            sub = tuple(t[1:])
            if sub == ("*",):
                for ent in g["subs"].values():
                    if ent[0] is not None:
                        deps.add(ent[0])
                if g["sw"] is not None:
                    deps.add(g["sw"])
                g["sr"].append(op.h)
            else:
                ent = g["subs"].get(sub)
                if ent is None:
                    ent = g["subs"][sub] = [None, []]
                if ent[0] is not None:
                    deps.add(ent[0])
                elif g["sw"] is not None:
                    deps.add(g["sw"])
                ent[1].append(op.h)
        for t in writes:
            g = self._grp(t[0])
            sub = tuple(t[1:])
            if sub == ("*",):
                for ent in g["subs"].values():
                    if ent[0] is not None:
                        deps.add(ent[0])
                    deps.update(ent[1])
                if g["sw"] is not None:
                    deps.add(g["sw"])
                deps.update(g["sr"])
                g["subs"] = {}
                g["sw"] = op.h
                g["sr"] = []
            else:
                ent = g["subs"].get(sub)
                if ent is None:
                    ent = g["subs"][sub] = [None, []]
                if ent[0] is not None:
                    deps.add(ent[0])
                elif g["sw"] is not None:
                    deps.add(g["sw"])
                deps.update(ent[1])
                deps.update(g["sr"])
                ent[0] = op.h
                ent[1] = []
        deps.discard(op.h)
        op.deps = deps
        self.q[eng].append(op)
        return op

    def finalize(self):
        for e in self.ENGS:
            for op in self.q[e]:
                for d in op.deps:
                    if d[0] == "c":
                        if e == "pe" and d[1] == "pe":
                            continue
                        self.q[d[1]][d[2]].sig = True
        self.rank = {}
        for e in self.ENGS:
            r = 0
            rk = []
            for op in self.q[e]:
                if op.sig:
                    r += 1
                rk.append(r)
            self.rank[e] = rk

    def emit_engine(self, e, eo, sem_eng, sem_dma, final_waits=()):
        waited = {}
        for op in self.q[e]:
            need = {}
            for d in op.deps:
                if d[0] == "c":
                    if e == "pe" and d[1] == "pe":
                        continue
                    key = ("c", d[1])
                    val = self.rank[d[1]][d[2]]
                else:
                    key = ("d", d[1])
                    val = 16 * d[2]
                if need.get(key, 0) < val:
                    need[key] = val
            for key, val in need.items():
                if waited.get(key, 0) < val:
                    sem = sem_eng[key[1]] if key[0] == "c" else sem_dma[key[1]]
                    eo.wait_ge(sem, val)
                    waited[key] = val
            ins = op.fn(eo)
            if op.dma is not None:
                ins.then_inc(sem_dma[op.dma], 16)
            elif op.sig:
                ins.then_inc(sem_eng[e], 1)
        for key in final_waits:
            eo.wait_ge(sem_dma[key], 16 * self.dma_n[key])


def build_nc():
    nc = bass.Bass("TRN2", target_bir_lowering=False)
    dr = {}

    def din(name, shape):
        dr[name] = nc.dram_tensor(name, list(shape), F32, kind="ExternalInput").ap()
        return dr[name]

    xT_d = din("xT", [NSEQ, 4, 128, 2, 2048])
    xtok_d = din("xtok", [NSEQ, S, D])
    win_d = din("w_in_r", [20, 128, 1024])
    wout_d = din("w_out_r", [128, 8, 1024])
    wffi_d = din("w_ffi_r", [NJ, 128, 2048])
    wffo_d = din("w_ffo_r", [128, NJ, 1024])
    chan_d = din("chanvec", [128, 36])
    lwa_d = din("lru_w_a", [8, 64, 64])
    lwx_d = din("lru_w_x", [8, 64, 64])
    dal_d = din("da_lambda", [1, 256])
    dag_d = din("da_norm_g", [1, 128])
    ln_d = [din(n, [1, 1024]) for n in ("ln1_g", "ln1_b", "ln2_g", "ln2_b")]
    out_d = nc.dram_tensor("out", [NSEQ, S, D], F32, kind="ExternalOutput").ap()

    P = Prog()

    R1N = 72 * 256
    R2N = 54 * 256
    R3N = 30 * 256
    with (
        nc.sbuf_tensor("R1", [128, R1N], F32) as R1,
        nc.sbuf_tensor("R2", [128, R2N], F32) as R2,
        nc.sbuf_tensor("R3", [128, R3N], F32) as R3,
        nc.sbuf_tensor("mg", [128, 8 * S], BF16) as mg_t,
        nc.sbuf_tensor("lnp", [128, 4 * 1024], F32) as lnp_t,
        nc.sbuf_tensor("ident", [128, 128], BF16) as ident_t,
        nc.sbuf_tensor("ones", [128, 128], BF16) as ones_t,
        nc.sbuf_tensor("tri2", [128, 256], BF16) as tri_t,
        nc.sbuf_tensor("chan", [128, 84], F32) as chan_t,
        nc.sbuf_tensor("wbda", [128, 4 * 128], BF16) as wbda_t,
        nc.sbuf_tensor("wbdx", [128, 4 * 128], BF16) as wbdx_t,
        nc.sbuf_tensor("gtile", [128, 128], F32) as gt_t,
        nc.psum_tensor("ps", [128, 8 * 512], F32) as ps_t,
    ):
        def view(arena, off_b, shape, dt):
            esz = 2 if dt == BF16 else 4
            n = 1
            for s_ in shape[1:]:
                n *= s_
            nb = n * esz
            assert off_b % 4 == 0 and nb % 4 == 0
            ap = arena[:, off_b // 4:(off_b + nb) // 4]
            if dt == BF16:
                ap = ap.bitcast(BF16)
            if len(shape) == 3:
                ap = ap.rearrange("p (a b) -> p a b", a=shape[1])
            elif len(shape) == 4:
                ap = ap.rearrange("p (a b c) -> p a b c", a=shape[1], b=shape[2])
            return ap

        KB = 1024
        xT = view(R1, 0, [128, 4, 8, 512], BF16)
        win = view(R1, 32 * KB, [128, 20, 8, 128], BF16)
        wout = view(R1, 0, [128, 8, 1024], BF16)
        wffo = view(R1, 16 * KB, [128, NJ, 1024], BF16)
        wsb = view(R1, 60 * KB, [128, 3, 2048], BF16)
        XP = 2052
        o = 0
        xpad = view(R2, o, [128, XP], F32); o += XP * 4
        xc = view(R2, o, [128, S], F32); o += S * 4
        ra = view(R2, o, [128, S], F32); o += S * 4
        ib = view(R2, o, [128, S], F32); o += S * 4
        mb = view(R2, o, [128, S], F32); o += S * 4
        gg = view(R2, o, [128, S], F32); o += S * 4
        xcb = view(R2, o, [128, S], BF16); o += S * 2
        assert o <= R2N * 4
        qkT = view(R2, 0, [128, 8, S], BF16)
        vaug = view(R2, 32 * KB, [128, 16, 4, 130], BF16)
        assert 32 * KB + 16 * 4 * 130 * 2 <= R2N * 4
        x1 = view(R2, 0, [128, 2, 4, 1024], F32)
        hT = view(R2, 32 * KB, [128, NJ, 512], BF16)
        Eb = view(R3, 0, [128, 3, 2, 512], BF16)
        of_ = view(R3, 6 * KB, [128, 4, 128], F32)
        onb = view(R3, 8 * KB, [128, 4, 128], BF16)
        junk = view(R3, 9 * KB, [128, 128], F32)
        ybuf = view(R3, 0, [128, 2, 1024], F32)
        osb = view(R3, 8 * KB, [128, 2, 1024], F32)
        x1b2 = view(R3, 28 * KB, [128, 1024], BF16)
        x1b = view(R3, 16 * KB, [128, 1024], BF16)
        sg = view(R3, 18 * KB, [128, 2, 512], BF16)
        xt = view(R3, 20 * KB, [128, 2, 1024], F32)
        mg = mg_t[:, :].rearrange("p (k t) -> p k t", k=8)
        lnp = lnp_t[:, :].rearrange("p (a d) -> p a d", a=4)
        ident = ident_t[:, :]
        ones = ones_t[:, :]
        tri2 = tri_t[:, :].rearrange("p (c j) -> p c j", c=2)
        chan = chan_t[:, 0:36].rearrange("p (c v) -> p c v", c=4)
        cneg = chan_t[:, 36:40]
        wbda = wbda_t[:, :].rearrange("p (c e) -> p c e", c=4)
        wbdx = wbdx_t[:, :].rearrange("p (c e) -> p c e", c=4)
        gtile = gt_t[:, :]
        lamt = view(R3, 10 * KB, [128, 256], F32)
        sm = chan_t[:, 40:84]
        neglam = sm[:, 0:1]
        s1 = sm[:, 1:2]
        s2 = sm[:, 2:3]
        zt = sm[:, 8:16]
        ss = sm[:, 16:20]
        rstd4 = sm[:, 20:24]
        bst = sm[:, 24:36]
        mv = sm[:, 36:38]
        lrstd = sm[:, 38:39]
        nmr = sm[:, 39:40]
        sp4 = sm[:, 40:44]
        cneg2 = sm[:, 3:7]
        ps = ps_t[:, :].rearrange("p (b n) -> p b n", b=8)
        psb4 = ps_t[:, 4 * 512:5 * 512].bitcast(BF16)
        psb7 = ps_t[:, 7 * 512:8 * 512].bitcast(BF16)

        def MG(G, k, t):
            return (("mg", G), k, t)

        def PS(b):
            return ("ps", b) if b < 5 else ("pb%d" % b, "*")

        def mm(out, lhsT, rhs, start, stop, reads, writes, skip=False):
            P.add("pe", lambda e: e.matmul(out, lhsT=lhsT, rhs=rhs, start=start, stop=stop,
                                           skip_group_check=skip), reads, writes)

        def tr(out, in_, reads, writes):
            P.add("pe", lambda e: e.transpose(out, in_, ident), reads, writes)

        def act(out, in_, func, reads, writes, bias=None, scale=None, accum_out=None):
            kw = {}
            if bias is not None:
                kw["bias"] = bias
            if scale is not None:
                kw["scale"] = scale
            if accum_out is not None:
                kw["accum_out"] = accum_out
            P.add("act", lambda e: e.activation(out=out, in_=in_, func=func, **kw), reads, writes)

        def ts(eng, out, in0, s1_, s2_, op0, op1, reads, writes):
            if op1 is None:
                P.add(eng, lambda e: e.tensor_scalar(out=out, in0=in0, scalar1=s1_, scalar2=None, op0=op0),
                      reads, writes)
            else:
                P.add(eng, lambda e: e.tensor_scalar(out=out, in0=in0, scalar1=s1_, scalar2=s2_, op0=op0, op1=op1),
                      reads, writes)

        def stt(out, in0, scalar, in1, op0, op1, reads, writes):
            P.add("dve", lambda e: e.scalar_tensor_tensor(out=out, in0=in0, scalar=scalar, in1=in1, op0=op0, op1=op1),
                  reads, writes)

        def tt(eng, out, in0, in1, op, reads, writes):
            P.add(eng, lambda e: e.tensor_tensor(out=out, in0=in0, in1=in1, op=op), reads, writes)

        def cp(eng, out, in_, reads, writes):
            if eng == "act":
                P.add("act", lambda e: e.copy(out=out, in_=in_), reads, writes)
            else:
                P.add(eng, lambda e: e.tensor_copy(out=out, in_=in_), reads, writes)

        def dma(eng, out, in_, key, reads, writes):
            P.add(eng, lambda e: e.dma_start(out=out, in_=in_), reads, writes, dma=key)

        def memset(eng, ap, val, reads, writes):
            P.add(eng, lambda e: e.memset(ap, val), reads, writes)

        I32 = mybir.dt.int32

        def rsqrt_dve(y, v, tmp, tok_y, tok_v, tok_t):
            vi = v.bitcast(I32)
            yi = y.bitcast(I32)
            P.add("dve", lambda e: e.tensor_single_scalar(out=yi, in_=vi, scalar=1, op=ALU.logical_shift_right),
                  [tok_v], [tok_y])
            P.add("dve", lambda e: e.tensor_scalar(out=yi, in0=yi, scalar1=-1.0, scalar2=float(0x5f3759df),
                                                   op0=ALU.mult, op1=ALU.add), [tok_y], [tok_y])
            for _ in range(3):
                tt("dve", tmp, y, y, ALU.mult, [tok_y], [tok_t])
                tt("dve", tmp, tmp, v, ALU.mult, [tok_t, tok_v], [tok_t])
                ts("dve", tmp, tmp, -0.5, 1.5, ALU.mult, ALU.add, [tok_t], [tok_t])
                tt("dve", y, y, tmp, ALU.mult, [tok_y, tok_t], [tok_y])

        memset("dve", ones, 1.0, [], [("c", "ones")])
        P.add("pool", lambda e: e.affine_select(out=ident, in_=ones, pattern=[[1, 128]], compare_op=ALU.is_equal,
                                                fill=0.0, base=0, channel_multiplier=-1),
              [("c", "ones")], [("c", "ident")])
        memset("pool", tri2[:, 0, :], 0.0, [], [("c", "tri0")])
        P.add("pool", lambda e: e.affine_select(out=tri2[:, 1, :], in_=tri2[:, 0, :], pattern=[[1, 128]],
                                                compare_op=ALU.is_ge, fill=-30000.0, base=0,
                                                channel_multiplier=-1),
              [("c", "tri0")], [("c", "tri")])
        dma("sp", chan_t[:, 0:36], chan_d[:, :], ("su", 0), [], [("c", "chan")])
        dma("sp", lamt, dal_d[0:1, :].partition_broadcast(128), ("su", 1), [], [("R3", "lamt")])
        dma("sp", gtile, dag_d[0:1, :].partition_broadcast(128), ("su", 2), [], [("c", "gtile")])
        for i in range(4):
            dma("sp", lnp[:, i, :], ln_d[i][0:1, :].partition_broadcast(128), ("su", 3 + i), [], [("c", "lnp", i)])
        act(sp4, chan[:, :, 7], AF.Exp, [("c", "chan")], [("c", "sp4")], scale=-1.0)
        act(sp4, sp4, AF.Ln, [("c", "sp4")], [("c", "sp4")], bias=1.0)
        ts("dve", cneg, sp4, -4.0, None, ALU.mult, None, [("c", "sp4")], [("c", "cneg")])
        ts("dve", cneg2, sp4, -8.0, None, ALU.mult, None, [("c", "sp4")], [("c", "cneg")])
        ts("dve", chan[:, :, 5:7], chan[:, :, 5:7], 0.5, None, ALU.mult, None, [("c", "chan")], [("c", "chan")])
        ts("dve", gtile, gtile, 1.0 - LAMBDA_INIT, None, ALU.mult, None, [("c", "gtile")], [("c", "gtile")])
        tt("dve", junk[:, 0:64], lamt[:, 0:64], lamt[:, 64:128], ALU.mult, [("R3", "lamt")], [("R3", "junk")])
        P.add("dve", lambda e: e.reduce_sum(out=s1, in_=junk[:, 0:64], axis=AX.X), [("R3", "junk")], [("c", "s1")])
        tt("dve", junk[:, 64:128], lamt[:, 128:192], lamt[:, 192:256], ALU.mult, [("R3", "lamt")], [("R3", "junk2")])
        P.add("dve", lambda e: e.reduce_sum(out=s2, in_=junk[:, 64:128], axis=AX.X), [("R3", "junk2")], [("c", "s2")])
        act(s1, s1, AF.Exp, [("c", "s1")], [("c", "s1")])
        act(s2, s2, AF.Exp, [("c", "s2")], [("c", "s2")])
        tt("dve", neglam, s2, s1, ALU.subtract, [("c", "s1"), ("c", "s2")], [("c", "neglam")])
        ts("dve", neglam, neglam, -LAMBDA_INIT, None, ALU.add, None, [("c", "neglam")], [("c", "neglam")])

        wst = view(R3, 20 * KB, [128, 2, 4, 128], F32)

        def load_wbd():
            memset("dve", wst, 0.0, [], [("R3", "wst", gi, n) for gi in range(2) for n in range(8)])
            for gi, src in enumerate((lwa_d, lwx_d)):
                for n in range(8):
                    cc, hf = n // 2, n % 2
                    dma("sp", wst[hf * 64:(hf + 1) * 64, gi, cc, hf * 64:(hf + 1) * 64], src[n, :, :],
                        ("su", 7 + 8 * gi + n), [], [("R3", "wst", gi, n)])

        def cast_wbd():
            cp("dve", wbda, wst[:, 0, :, :], [("R3", "wst", 0, n) for n in range(8)], [("c", "wbda")])
            cp("dve", wbdx, wst[:, 1, :, :], [("R3", "wst", 1, n) for n in range(8)], [("c", "wbdx")])

        rot = {"a2": 0, "a1": 0, "ev": 0}

        def load_win(blk, star=False):
            w = [("R1", "*")] if star else [("R1", "win", blk)]
            dma("pool", win[:, blk, :, :].rearrange("p k c -> p (k c)"), win_d[blk], ("win", blk), [], w)

        def load_xT(s, tg, star=False, alias=()):
            w = [("R1", "*")] if star else [("R1", "xT", tg)] + list(alias)
            dma("pool", xT[:, tg, :, :].rearrange("p (a k) t -> p a (k t)", a=2), xT_d[s, tg, :, :, :],
                ("xT", tg), [], w)

        def load_seq_inputs(s):
            if s == 0:
                load_xT(s, 0)
                load_win(12)
                load_win(16)
                load_xT(s, 1)
            else:
                load_xT(s, 0, alias=[("R1", "wout", k) for k in range(0, 4)])
                load_xT(s, 1, alias=[("R1", "wout", k) for k in range(4, 8)])
                load_win(12, star=True)
                load_win(16)
            for tg in range(2, 4):
                load_xT(s, tg)
            for cc in range(1, 4):
                load_win(12 + cc)
                load_win(16 + cc)
            for blk in range(12):
                load_win(blk)

        def phase_lru(s):
            gg2 = view(R3, 0, [128, 2, S], F32)
            sqb = view(R3, 16 * KB, [128, S], BF16)
            r3_first = [True]
            TS = [slice(tg * 512, (tg + 1) * 512) for tg in range(4)]
            memset("dve", xpad[:, 0:3], 0.0, [], [("R2", "*")])

            def proj(blk, tg, b):
                for k in range(8):
                    mm(ps[:, b, :], win[:, blk, k, :], xT[:, tg, k, :], k == 0, k == 7,
                       [("R1", "win", blk), ("R1", "xT", tg)], [PS(b)])

            PB = [0, 1, 4, 5, 6]

            def head(cc):
                par = cc % 2
                for tg in range(4):
                    b = PB[rot["a2"] % 5]
                    rot["a2"] += 1
                    proj(12 + cc, tg, b)
                    cp("act", xpad[:, 3 + tg * 512:3 + (tg + 1) * 512], ps[:, b, :], [PS(b)], [("R2", "xpad", tg)])
                for tg in range(4):
                    b = PB[rot["a2"] % 5]
                    rot["a2"] += 1
                    proj(16 + cc, tg, b)
                    w = [("R3", "*")] if r3_first[0] else [("R3", "gg", par, tg)]
                    r3_first[0] = False
                    act(gg2[:, par, TS[tg]], ps[:, b, :], AF.Gelu_apprx_tanh, [PS(b)], w)

            def conv_gates(cc, tg):
                tsl = TS[tg]
                xr = [("R2", "xpad", tg)] + ([("R2", "xpad", tg - 1)] if tg > 0 else [])
                ts("dve", xc[:, tsl], xpad[:, 3 + tg * 512:3 + (tg + 1) * 512], chan[:, cc, 3:4], chan[:, cc, 4:5],
                   ALU.mult, ALU.add, xr + [("c", "chan")], [("R2", "xc", tg)])
                for tap in (2, 1, 0):
                    stt(xc[:, tsl], xpad[:, tap + tg * 512:tap + (tg + 1) * 512], chan[:, cc, tap:tap + 1],
                        xc[:, tsl], ALU.mult, ALU.add, xr + [("R2", "xc", tg)], [("R2", "xc", tg)])
                cp("act", xcb[:, tsl], xc[:, tsl], [("R2", "xc", tg)], [("R2", "xcb", tg)])
                mm(ps[:, 2, :], wbda[:, cc, :], xcb[:, tsl], True, True, [("c", "wbda"), ("R2", "xcb", tg)], [PS(2)])
                act(ra[:, tsl], ps[:, 2, :], AF.Tanh, [PS(2), ("c", "chan")], [("R2", "ra", tg)],
                    bias=chan[:, cc, 5:6], scale=0.5)
                mm(ps[:, 3, :], wbdx[:, cc, :], xcb[:, tsl], True, True, [("c", "wbdx"), ("R2", "xcb", tg)], [PS(3)])
                act(ib[:, tsl], ps[:, 3, :], AF.Tanh, [PS(3), ("c", "chan")], [("R2", "ib", tg)],
                    bias=chan[:, cc, 6:7], scale=0.5)
                act(mb[:, tsl], ra[:, tsl], AF.Exp, [("R2", "ra", tg), ("c", "cneg")], [("R2", "mb", tg)],
                    bias=cneg2[:, cc:cc + 1], scale=cneg2[:, cc:cc + 1])
                act(ra[:, tsl], ra[:, tsl], AF.Exp, [("R2", "ra", tg), ("c", "cneg")], [("R2", "ra", tg)],
                    bias=cneg[:, cc:cc + 1], scale=cneg[:, cc:cc + 1])

            def tail_a(cc):
                for tg in range(4):
                    tsl = TS[tg]
                    act(mb[:, tsl], mb[:, tsl], AF.Sqrt, [("R2", "mb", tg)], [("R2", "mb", tg)], bias=1.0, scale=-1.0)
                memset("dve", mb[:, 0:1], 1.0, [("R2", "mb", 0)], [("R2", "mb", 0)])

            carry = sm[:, 40:41]

            def tail_tg(cc, tg):
                par = cc % 2
                tsl = TS[tg]
                stt(ib[:, tsl], ib[:, tsl], 1.0, xc[:, tsl], ALU.add, ALU.mult,
                    [("R2", "ib", tg), ("R2", "xc", tg)], [("R2", "ib", tg)])
                stt(ib[:, tsl], ib[:, tsl], 0.5, mb[:, tsl], ALU.mult, ALU.mult,
                    [("R2", "ib", tg), ("R2", "mb", tg)], [("R2", "ib", tg)])
                if tg == 0:
                    P.add("dve", lambda e: e.tensor_tensor_scan(
                        out=xc[:, tsl], data0=ra[:, tsl], data1=ib[:, tsl], initial=0.0,
                        op0=ALU.mult, op1=ALU.add),
                        [("R2", "ra", tg), ("R2", "ib", tg), ("R2", "xc", tg)], [("R2", "xc", tg)])
                else:
                    P.add("dve", lambda e: e.tensor_tensor_scan(
                        out=xc[:, tsl], data0=ra[:, tsl], data1=ib[:, tsl], initial=carry,
                        op0=ALU.mult, op1=ALU.add),
                        [("R2", "ra", tg), ("R2", "ib", tg), ("R2", "xc", tg), ("c", "carry")], [("R2", "xc", tg)])
                if tg < 3:
                    cp("dve", carry, xc[:, (tg + 1) * 512 - 1:(tg + 1) * 512], [("R2", "xc", tg)], [("c", "carry")])
                mgw = [MG(tg, 4 + cc, t) for t in range(4)]
                tt("dve", mg[:, 4 + cc, tsl], xc[:, tsl], gg2[:, par, tsl], ALU.mult,
                   [("R2", "xc", tg), ("R3", "gg", par, tg)], mgw)

            head(0)
            if s == 0:
                cast_wbd()
            for tg in range(4):
                conv_gates(0, tg)
            for cc in range(4):
                tail_a(cc)
                if cc + 1 < 4:
                    head(cc + 1)
                for tg in range(4):
                    tail_tg(cc, tg)
                    if cc + 1 < 4:
                        conv_gates(cc + 1, tg)
            rs = gg2[:, 0, :]
            units = []

            def u_sq(tg, cc):
                def f():
                    tsl = TS[tg]
                    mgw = [MG(tg, 4 + cc, t) for t in range(4)]
                    act(sqb[:, TS[cc]], mg[:, 4 + cc, tsl], AF.Square, mgw, [("R3", "sq", cc)])
                    mm(ps[:, 4 + tg, :], ones, sqb[:, TS[cc]], cc == 0, cc == 3, [("c", "ones"), ("R3", "sq", cc)],
                       [PS(4 + tg)])
                return f

            def u_rs(tg):
                def f():
                    tsl = TS[tg]
                    act(rs[:, tsl], ps[:, 4 + tg, :], AF.Sqrt, [PS(4 + tg)], [("R3", "gg", 0, tg)],
                        bias=RMS_EPS, scale=1.0 / 512.0)
                    P.add("dve", lambda e: e.reciprocal(out=rs[:, tsl], in_=rs[:, tsl]),
                          [("R3", "gg", 0, tg)], [("R3", "gg", 0, tg)])
                return f

            def u_scale(tg, cc):
                def f():
                    tsl = TS[tg]
                    mgw = [MG(tg, 4 + cc, t) for t in range(4)]
                    stt(mg[:, 4 + cc, tsl], mg[:, 4 + cc, tsl], chan[:, cc, 8:9], rs[:, tsl], ALU.mult, ALU.mult,
                        mgw + [("R3", "gg", 0, tg), ("c", "chan")], mgw)
                return f

            for tg in range(4):
                for cc in range(4):
                    units.append(u_sq(tg, cc))
            for tg in range(4):
                units.append(u_rs(tg))
                for cc in range(4):
                    units.append(u_scale(tg, cc))
            return units

        def phase_qkv(s, units):
            memset("dve", vaug[:, :, :, 128:129], 1.0, [], [("R2", "*")])
            for ch in range(8):
                for tg in range(4):
                    tsl = slice(tg * 512, (tg + 1) * 512)
                    b = rot["a1"] % 4
                    rot["a1"] += 1
                    for k in range(8):
                        mm(ps[:, b, :], win[:, ch, k, :], xT[:, tg, k, :], k == 0, k == 7,
                           [("R1", "win", ch), ("R1", "xT", tg)], [PS(b)])
                    eng = "act" if rot["ev"] % 2 == 0 else "dve"
                    rot["ev"] += 1
                    cp(eng, qkT[:, ch, tsl], ps[:, b, :], [PS(b)], [("R2", "qk", ch, tg)])
                    if units:
                        units.pop(0)()
            for t16 in range(16):
                b = rot["a1"] % 4
                rot["a1"] += 1
                for k in range(8):
                    mm(ps[:, b, :].rearrange("p (h e) -> p h e", h=4),
                       xT[:, t16 // 4, k, (t16 % 4) * 128:(t16 % 4 + 1) * 128],
                       win[:, 8:12, k, :], k == 0, k == 7,
                       [("R1", "win", 8 + i) for i in range(4)] + [("R1", "xT", t16 // 4)], [PS(b)])
                eng = "act" if rot["ev"] % 2 == 0 else "dve"
                rot["ev"] += 1
                cp(eng, vaug[:, t16, :, 0:128], ps[:, b, :].rearrange("p (h e) -> p h e", h=4), [PS(b)],
                   [("R2", "v", t16)])
                if units:
                    units.pop(0)()
            while units:
                units.pop(0)()

        def load_cd_weights(s):
            for k in range(8):
                w = [("R1", "*")] if k == 0 else [("R1", "wout", k)]
                dma("pool", wout[:, k, :], wout_d[:, k, :], ("wout", k), [], w)
            for jj in range(NJ // 2):
                dma("pool", wffo[:, 2 * jj:2 * jj + 2, :], wffo_d[:, 2 * jj:2 * jj + 2, :], ("wffo", jj), [],
                    [("R1", "wffo", jj)])

        def acc_ap(a, lo, hi):
            b = 5 + a // 3
            off = (a % 3) * 129
            return ps[:, b, off + lo:off + hi]

        def acc_tok(a):
            return ("pb%d" % (5 + a // 3), a)

        def phase_attn(s):
            first_e = [True]
            steps = []
            for h in range(4):
                for qg in range(4):
                    nfull = 4 * qg
                    lst = [(kb, -1) for kb in range(nfull)] + [(nfull + i, i) for i in range(4)]
                    for si, (kb, i) in enumerate(lst):
                        steps.append((h, qg, kb, i, si == 0, si == len(lst) - 1))
            ecnt = [0]

            def emit_qk(n):
                h, qg, kb, i, _, _ = steps[n]
                sp_ = n % 2
                col0 = 0 if i < 0 else i * 128
                qtok = [("R2", "qk", 4 + h, kb // 4), ("R2", "qk", h, qg)]
                for c in range(2):
                    kT_c = qkT[c * 64:(c + 1) * 64, 4 + h, kb * 128:(kb + 1) * 128]
                    bank = 2 * sp_ + c
                    if i < 0:
                        mm(ps[:, bank, 0:512], kT_c, qkT[c * 64:(c + 1) * 64, h, qg * 512:(qg + 1) * 512],
                           True, True, qtok, [("ps", bank)])
                    else:
                        q0 = qg * 512 + col0
                        mm(ps[:, bank, col0:col0 + 128], kT_c, qkT[c * 64:(c + 1) * 64, h, q0:q0 + 128],
                           True, False, qtok, [("ps", bank)])
                        mm(ps[:, bank, col0:col0 + 128], ident, tri2[:, 1, :], False, True,
                           [("c", "ident"), ("c", "tri")], [("ps", bank)])
                        if col0 + 128 < 512:
                            mm(ps[:, bank, col0 + 128:512], kT_c,
                               qkT[c * 64:(c + 1) * 64, h, q0 + 128:(qg + 1) * 512],
                               True, True, qtok, [("ps", bank)])

            def emit_exp(n):
                h, qg, kb, i, _, _ = steps[n]
                sp_ = n % 2
                eb = n % 3
                col0 = 0 if i < 0 else i * 128
                w = [("R3", "*")] if first_e[0] else [("R3", "E", eb)]
                first_e[0] = False
                act(Eb[:, eb, :, col0:512], ps[:, 2 * sp_:2 * sp_ + 2, col0:512], AF.Exp,
                    [("ps", 2 * sp_), ("ps", 2 * sp_ + 1)], w, scale=0.125)

            def emit_pv(n):
                h, qg, kb, i, is_first, is_last = steps[n]
                eb = n % 3
                qb0 = 0 if i < 0 else i
                for qb in range(qb0, 4):
                    for c in range(2):
                        a = qb * 2 + c
                        st = is_first and (a % 3 == 0)
                        sp = (kb == 4 * qg + qb)
                        mm(acc_ap(a, 0, 129), Eb[:, eb, c, qb * 128:(qb + 1) * 128], vaug[:, kb, h, 0:129],
                           st, sp, [("R3", "E", eb), ("R2", "v", kb)], [acc_tok(a)], skip=True)
                if is_last:
                    finalize(h, qg, n)
                while pend and pend[0][0] <= n:
                    fin_b(*pend.pop(0)[1])

            accsb = view(R3, 10 * KB, [128, 3, 387], F32)
            onb2 = view(R3, 16 * KB, [128, 2, 4, 128], BF16)
            pend = []
            gcnt = [0]

            def finalize(h, qg, n):
                while len(pend) > 1:
                    fin_b(*pend.pop(0)[1])
                gp = gcnt[0] % 2
                gcnt[0] += 1
                for bi in range(3):
                    na = 3 if bi < 2 else 2
                    cp("dve", accsb[:, bi, 0:na * 129], ps[:, 5 + bi, 0:na * 129],
                       [acc_tok(3 * bi + j) for j in range(na)], [("R3", "accsb", bi)])
                av = accsb.rearrange("p b (a c) -> p b a c", c=129)
                zt4 = zt.rearrange("p (a o) -> p a o", o=1)
                P.add("dve", lambda e: e.reciprocal(out=zt[:, 0:6].rearrange("p (b a o) -> p b a o", b=2, o=1),
                                                    in_=av[:, 0:2, :, 128:129]),
                      [("R3", "accsb", 0), ("R3", "accsb", 1)], [("c", "zt")])
                P.add("dve", lambda e: e.reciprocal(out=zt4[:, 6:8, :], in_=av[:, 2, 0:2, 128:129]),
                      [("R3", "accsb", 2)], [("c", "zt")])
                ztv = zt.rearrange("p (q c) -> p q c", c=2)
                ts("dve", ztv[:, :, 1:2], ztv[:, :, 1:2], neglam, None, ALU.mult, None,
                   [("c", "zt"), ("c", "neglam")], [("c", "zt")])

                def acs(a):
                    return accsb[:, a // 3, (a % 3) * 129:(a % 3) * 129 + 128]

                for qb in range(4):
                    rd = [("R3", "accsb", (2 * qb) // 3), ("R3", "accsb", (2 * qb + 1) // 3), ("c", "zt")]
                    ts("dve", of_[:, qb, :], acs(2 * qb), zt[:, 2 * qb:2 * qb + 1], None, ALU.mult, None,
                       rd, [("R3", "of", qb)])
                    stt(of_[:, qb, :], acs(2 * qb + 1), zt[:, 2 * qb + 1:2 * qb + 2], of_[:, qb, :],
                        ALU.mult, ALU.add, rd + [("R3", "of", qb)], [("R3", "of", qb)])
                    P.add("dve", lambda e, qb=qb: e.scalar_tensor_tensor(
                        out=junk, in0=of_[:, qb, :], scalar=1.0, in1=of_[:, qb, :], op0=ALU.mult, op1=ALU.mult,
                        accum_out=ss[:, qb:qb + 1]), [("R3", "of", qb)], [("R3", "junk"), ("c", "ss")])
                ts("dve", ss, ss, 1.0 / 128.0, RMS_EPS, ALU.mult, ALU.add, [("c", "ss")], [("c", "ss")])
                rsqrt_dve(rstd4, ss, zt[:, 0:4], ("c", "rstd4"), ("c", "ss"), ("c", "zt"))
                for qb in range(4):
                    stt(onb2[:, gp, qb, :], of_[:, qb, :], rstd4[:, qb:qb + 1], gtile, ALU.mult, ALU.mult,
                        [("R3", "of", qb), ("c", "rstd4"), ("c", "gtile")], [("R3", "onb", gp, qb)])
                pend.append((n + 14, (h, qg, gp)))

            def fin_b(h, qg, gp):
                for qb in range(4):
                    tr(psb4[:, qb * 128:(qb + 1) * 128], onb2[:, gp, qb, :],
                       [("R3", "onb", gp, qb), ("c", "ident")], [PS(4)])
                cp("dve", mg[:, h, qg * 512:(qg + 1) * 512], psb4[:, 0:512], [PS(4)],
                   [MG(qg, h, t) for t in range(4)])

            N = len(steps)
            emit_qk(0)
            for n in range(N):
                emit_exp(n)
                if n + 1 < N:
                    emit_qk(n + 1)
                emit_pv(n)
            while pend:
                fin_b(*pend.pop(0)[1])

        def lnsc(p):
            if p == 0:
                return dict(bst=sm[:, 24:36], mv=sm[:, 36:38], lr=sm[:, 38:39], nm=sm[:, 39:40], tmp=sm[:, 1:2],
                            t_bst=[("c", "bst0")], t_mv=("c", "mv0"), t_lr=("c", "lr0"), t_nm=("c", "nm0"),
                            t_tmp=("c", "tmp0"))
            return dict(bst=sm[:, 8:20], mv=sm[:, 20:22], lr=sm[:, 22:23], nm=sm[:, 23:24], tmp=sm[:, 7:8],
                        t_bst=[("c", "zt"), ("c", "ss")], t_mv=("c", "rstd4"), t_lr=("c", "rstd4"),
                        t_nm=("c", "rstd4"), t_tmp=("c", "tmp1"))

        def rsqrt_col(y, v, tmp, tok_y, tok_v, tok_t):
            vi = v.bitcast(I32)
            yi = y.bitcast(I32)
            P.add("dve", lambda e: e.tensor_single_scalar(out=yi, in_=vi, scalar=1, op=ALU.logical_shift_right),
                  [tok_v], [tok_y])
            P.add("dve", lambda e: e.tensor_scalar(out=yi, in0=yi, scalar1=-1.0, scalar2=float(0x5f3759df),
                                                   op0=ALU.mult, op1=ALU.add), [tok_y], [tok_y])
            for _ in range(3):
                stt(tmp, y, v, y, ALU.mult, ALU.mult, [tok_y, tok_v], [tok_t])
                ts("dve", tmp, tmp, -0.5, 1.5, ALU.mult, ALU.add, [tok_t], [tok_t])
                tt("dve", y, y, tmp, ALU.mult, [tok_y, tok_t], [tok_y])

        def ln_stats(yb, y_tok, sc):
            for c in range(2):
                P.add("dve", lambda e, c=c: e.bn_stats(out=sc["bst"][:, 6 * c:6 * c + 6],
                                                       in_=yb[:, c * 512:(c + 1) * 512]),
                      [y_tok], sc["t_bst"])
            P.add("dve", lambda e: e.bn_aggr(out=sc["mv"], in_=sc["bst"]), sc["t_bst"], [sc["t_mv"]])
            ts("dve", sc["mv"][:, 1:2], sc["mv"][:, 1:2], LN_EPS, None, ALU.add, None, [sc["t_mv"]], [sc["t_mv"]])

        def ln_rstd(sc):
            rsqrt_col(sc["lr"], sc["mv"][:, 1:2], sc["tmp"], sc["t_lr"], sc["t_mv"], sc["t_tmp"])
            stt(sc["nm"], sc["mv"][:, 0:1], -1.0, sc["lr"], ALU.mult, ALU.mult, [sc["t_mv"], sc["t_lr"]],
                [sc["t_nm"]])

        def ln_norm(yb, y_tok, sc):
            act(yb, yb, AF.Identity, [y_tok, sc["t_lr"], sc["t_nm"]], [y_tok], bias=sc["nm"], scale=sc["lr"])

        def ln_affine(yb, y_tok, gi, bi_, dst, dst_tok, eng="dve"):
            tt(eng, dst, yb, lnp[:, gi, :], ALU.mult, [y_tok, ("c", "lnp", gi)], [dst_tok])
            tt(eng, dst, dst, lnp[:, bi_, :], ALU.add, [dst_tok, ("c", "lnp", bi_)], [dst_tok])

        def layer_norm(yb, gi, bi_, dst, y_tok, dst_tok, sc):
            ln_stats(yb, y_tok, sc)
            ln_rstd(sc)
            ln_norm(yb, y_tok, sc)
            ln_affine(yb, y_tok, gi, bi_, dst, dst_tok)

        cnt = {"xt": 0, "y": 0, "o": 0, "ws": 0, "gu": 0, "sg": 0, "acc": 0}

        def phase_cd(s):
            st = {"r2_first": True, "r3_first": True}

            ctxs = {}

            def c_stage(G, t, d):
                t16 = 4 * G + t
                par = G % 2
                if d == 0:
                    xs = cnt["xt"] % 2
                    cnt["xt"] += 1
                    w = [("R3", "*")] if st["r3_first"] else [("R3", "xt", xs)]
                    st["r3_first"] = False
                    dma("sp", xt[:, xs, :], xtok_d[s, t16 * 128:(t16 + 1) * 128, :], ("xt", xs), [], w)
                    ab = 2 * (cnt["acc"] % 2)
                    cnt["acc"] += 1
                    for hf in range(2):
                        for k in range(8):
                            mm(ps[:, ab + hf, :], mg[:, k, t16 * 128:(t16 + 1) * 128],
                               wout[:, k, hf * 512:(hf + 1) * 512], k == 0, k == 7,
                               [MG(G, k, t), ("R1", "wout", k)], [PS(ab + hf)])
                    ctxs[(G, t)] = dict(xs=xs, ab=ab, ys=t % 2, sc=lnsc(t % 2))
                    return
                c = ctxs[(G, t)]
                ys = c["ys"]
                yb = ybuf[:, ys, :]
                ytok = ("R3", "y", ys)
                if d == 1:
                    ab = c["ab"]
                    stt(yb, xt[:, c["xs"], :], ALPHA, ps_t[:, ab * 512:(ab + 2) * 512], ALU.mult, ALU.add,
                        [("R3", "xt", c["xs"]), PS(ab), PS(ab + 1)], [ytok])
                    ln_stats(yb, ytok, c["sc"])
                elif d == 2:
                    ln_rstd(c["sc"])
                elif d == 3:
                    ln_norm(yb, ytok, c["sc"])
                elif d == 4:
                    dst_tok = ("R2", "*") if st["r2_first"] else ("R2", "x1", par, t)
                    st["r2_first"] = False
                    ln_affine(yb, ytok, 0, 1, x1[:, par, t, :], dst_tok, eng="pool" if G == 0 else "dve")
                elif d == 5:
                    xb_ = x1b if t % 2 == 0 else x1b2
                    cp("act", xb_, x1[:, par, t, :], [("R2", "x1", par, t)], [("R3", "x1b", t % 2)])
                elif d == 6:
                    xb_ = x1b if t % 2 == 0 else x1b2
                    for k in range(8):
                        tr(psb7[:, k * 128:(k + 1) * 128], xb_[:, k * 128:(k + 1) * 128],
                           [("R3", "x1b", t % 2), ("c", "ident")], [PS(7)])
                    cp("act" if G == 0 else "dve", mg[:, :, t16 * 128:(t16 + 1) * 128],
                       psb7.rearrange("p (k t) -> p k t", k=8), [PS(7)], [MG(G, k, t) for k in range(8)])

            done_ = set()
            for t_, d_ in ((0, 0), (1, 0), (0, 1), (2, 0), (1, 1), (3, 0)):
                c_stage(0, t_, d_)
                done_.add((t_, d_))
            for v in range(0, 6 + 2 * 3 + 1):
                for t in range(4):
                    d = v - 2 * t
                    if 0 <= d <= 6 and (t, d) not in done_:
                        c_stage(0, t, d)
            ln2_def = {}
            for G in range(4):
                par = G % 2
                x1T_r = [(("mg", G), "*")]
                for j in range(NJ):
                    wsl = cnt["ws"] % 3
                    cnt["ws"] += 1
                    dma("pool", wsb[:, wsl, :], wffi_d[j, :, :], ("ws", wsl), [], [("R1", "ws", wsl)])
                    wv = wsb[:, wsl, :].rearrange("p (k c) -> p k c", k=8)
                    gbk = []
                    for u in range(2):
                        gb = 4 + (cnt["gu"] % 3)
                        cnt["gu"] += 1
                        gbk.append(gb)
                        for k in range(8):
                            mm(ps[:, gb, :], wv[:, k, u * 128:(u + 1) * 128], mg[:, k, G * 512:(G + 1) * 512],
                               k == 0, k == 7, [("R1", "ws", wsl)] + x1T_r, [PS(gb)])
                    sgs = cnt["sg"] % 2
                    cnt["sg"] += 1
                    act(sg[:, sgs, :], ps[:, gbk[0], :], AF.Silu, [PS(gbk[0])], [("R3", "sg", sgs)])
                    tt("dve", hT[:, j, :], sg[:, sgs, :], ps[:, gbk[1], :], ALU.mult,
                       [("R3", "sg", sgs), PS(gbk[1])], [("R2", "hT", j)])
                    for f_ in ln2_def.pop(j, []):
                        f_()
                    if G + 1 < 4:
                        for t in range(4):
                            d = j - 5 * t
                            if 0 <= d <= 6:
                                c_stage(G + 1, t, d)
                for t in range(4):
                    t16 = 4 * G + t
                    ab = 2 * (cnt["acc"] % 2)
                    cnt["acc"] += 1
                    for hf in range(2):
                        for j in range(NJ):
                            mm(ps[:, ab + hf, :], hT[:, j, t * 128:(t + 1) * 128],
                               wffo[:, j, hf * 512:(hf + 1) * 512], j == 0, j == NJ - 1,
                               [("R2", "hT", j), ("R1", "wffo", j // 2)], [PS(ab + hf)])
                    ys = t % 2
                    os_ = cnt["o"] % 2
                    cnt["o"] += 1
                    sc_ = lnsc(t % 2)

                    def l0(ys=ys, ab=ab, par=par, t=t, sc_=sc_):
                        stt(ybuf[:, ys, :], x1[:, par, t, :], ALPHA, ps_t[:, ab * 512:(ab + 2) * 512],
                            ALU.mult, ALU.add, [("R2", "x1", par, t), PS(ab), PS(ab + 1)], [("R3", "y", ys)])
                        ln_stats(ybuf[:, ys, :], ("R3", "y", ys), sc_)

                    def l1(sc_=sc_):
                        ln_rstd(sc_)

                    def l2(ys=ys, sc_=sc_):
                        ln_norm(ybuf[:, ys, :], ("R3", "y", ys), sc_)

                    def l3(ys=ys, os_=os_, t16=t16):
                        ln_affine(ybuf[:, ys, :], ("R3", "y", ys), 2, 3, osb[:, os_, :], ("R3", "o", os_))
                        dma("sp", out_d[s, t16 * 128:(t16 + 1) * 128, :], osb[:, os_, :], ("st", os_),
                            [("R3", "o", os_)], [])

                    if t == 3 and G < 3:
                        ln2_def[0] = [l0]
                        ln2_def[1] = [l1]
                        ln2_def[2] = [l2]
                        ln2_def[3] = [l3]
                    else:
                        l0()
                        l1()
                        l2()
                        l3()

        for s in range(NSEQ):
            if s == 0:
                load_wbd()
            load_seq_inputs(s)
            units = phase_lru(s)
            phase_qkv(s, units)
            load_cd_weights(s)
            phase_attn(s)
            phase_cd(s)

        P.finalize()
        keys = sorted(P.dma_n.keys(), key=str)
        sem_eng = {}
        sem_dma = {}
        for e in ("pe", "act", "dve", "pool"):
            sem_eng[e] = nc.alloc_semaphore(name="se_" + e)
        for i, k in enumerate(keys):
            sem_dma[k] = nc.alloc_semaphore(name="sd_%d" % i)
        with nc.Block() as block:
            @block.tensor
            def _(e):
                P.emit_engine("pe", e, sem_eng, sem_dma)

            @block.scalar
            def _(e):
                P.emit_engine("act", e, sem_eng, sem_dma)

            @block.vector
            def _(e):
                P.emit_engine("dve", e, sem_eng, sem_dma)

            @block.gpsimd
            def _(e):
                P.emit_engine("pool", e, sem_eng, sem_dma)

            @block.sync
            def _(e):
                P.emit_engine("sp", e, sem_eng, sem_dma, final_waits=[("st", 0), ("st", 1)])
    return nc


def _prep_shared(inp):
    f = lambda a: np.ascontiguousarray(np.asarray(a, dtype=np.float32))
    w_in = f(inp["w_in"])[0]
    w_out = f(inp["w_out"])[0]
    w_ffi = f(inp["w_ffn_in"])[0]
    w_ffo = f(inp["w_ffn_out"])[0]
    sh = {}
    sh["w_in_r"] = f(w_in.reshape(8, 128, 20, 128).transpose(2, 1, 0, 3).reshape(20, 128, 1024))
    sh["w_out_r"] = f(w_out.reshape(8, 128, 1024).transpose(1, 0, 2))
    g = w_ffi[:, :DFF].reshape(8, 128, NJ, 128)
    u = w_ffi[:, DFF:].reshape(8, 128, NJ, 128)
    gu = np.concatenate([g, u], axis=3)
    sh["w_ffi_r"] = f(gu.transpose(2, 1, 0, 3).reshape(NJ, 128, 2048))
    sh["w_ffo_r"] = f(w_ffo.reshape(NJ, 128, 1024).transpose(1, 0, 2))
    cv = np.zeros((128, 4, 9), np.float32)
    conv_w = f(inp["conv_w"])[0]
    for tap in range(4):
        cv[:, :, tap] = conv_w[tap].reshape(4, 128).T
    for i, n in enumerate(["conv_b", "lru_b_a", "lru_b_x", "lru_lambda", "rec_norm_g"]):
        cv[:, :, 4 + i] = f(inp[n])[0].reshape(4, 128).T
    sh["chanvec"] = f(cv.reshape(128, 36))
    sh["lru_w_a"] = f(inp["lru_w_a"])[0]
    sh["lru_w_x"] = f(inp["lru_w_x"])[0]
    sh["da_lambda"] = f(inp["da_lambda"])[0].reshape(1, 256)
    sh["da_norm_g"] = f(inp["da_norm_g"])[0].reshape(1, 128)
    for n in ("ln1_g", "ln1_b", "ln2_g", "ln2_b"):
        sh[n] = f(inp[n])[0].reshape(1, 1024)
    return sh


def kernel(**inputs):
    x = np.asarray(inputs["x"], dtype=np.float32)
    sh = _prep_shared(inputs)
    in_maps = []
    for c in range(NCORES):
        xc = x[c * NSEQ:(c + 1) * NSEQ]
        xT = np.ascontiguousarray(xc.reshape(NSEQ, 4, 512, 8, 128).transpose(0, 1, 4, 3, 2)).reshape(
            NSEQ, 4, 128, 2, 2048)
        m = dict(sh)
        m["xT"] = xT
        m["xtok"] = np.ascontiguousarray(xc)
        in_maps.append(m)
    nc = build_nc()
    res = run_bass_kernel_spmd(nc, in_maps, core_ids=list(range(NCORES)))
    out = np.concatenate([np.asarray(r["out"], dtype=np.float32) for r in res.results], axis=0)
    return out


if __name__ == "__main__":
    import time
    t0 = time.time()
    nc = build_nc()
    print("build ok", time.time() - t0)
```
